# Optimizing a Trainium2 kernel written in Bass

```python
import jax, jax.numpy as jnp
from jax import lax
import numpy as np

D_MODEL = 1024
BATCH = 8
SEQ = 2048
DEPTH = 1
DEC_BATCH = 128
DEC_SEQ = 1
PAST_LEN = 16384
PAGE_SIZE = 128

D_A = D_MODEL // 2
CHUNK = 128
N_HEADS_A = 4
HEAD_A = D_A // N_HEADS_A
D_B = D_MODEL - D_A
HEAD_B = 64
N_HEADS_B = D_B // HEAD_B
D_DECAY_LORA = 64
D_AAA_LORA = 64
D_GATE_LORA = 128
D_SHIFT = 3 * D_B + D_DECAY_LORA + D_AAA_LORA + D_GATE_LORA
D_IN = 2 * D_A + D_SHIFT
N_MEM = 256
N_HEADS_X = 4
HEAD_X = D_MODEL // N_HEADS_X
D_FF = 4 * D_MODEL
RMS_EPS = 1e-6
LN_EPS = 1e-5
GN_EPS = 64e-5

kernel_name = "hybrid_chunkgmlp_rwkv7_memxattn_step"

F32 = jnp.float32


def rmsnorm(x, g):
    x32 = x.astype(F32)
    y = x32 * lax.rsqrt(jnp.mean(x32 * x32, axis=-1, keepdims=True) + RMS_EPS)
    return (y * g.astype(F32)).astype(x.dtype)


def headnorm(x, g, b, eps):
    H, P = x.shape[-2:]
    x32 = x.astype(F32)
    mu = jnp.mean(x32, axis=-1, keepdims=True)
    xc = x32 - mu
    var = jnp.mean(xc * xc, axis=-1, keepdims=True)
    y = xc * lax.rsqrt(var + eps)
    return (y * g.astype(F32).reshape(H, P) + b.astype(F32).reshape(H, P)).astype(x.dtype)


def chunk_spatial_gate(u, vn, ws, bs):
    bsz, L, H, P = u.shape
    n_chunks = -(-L // CHUNK)
    pad = n_chunks * CHUNK - L
    vp = jnp.pad(vn, ((0, 0), (0, pad), (0, 0), (0, 0))).reshape(bsz, n_chunks, CHUNK, H, P)
    mask = jnp.tril(jnp.ones((CHUNK, CHUNK), dtype=bool))
    ws_c = jnp.where(mask[None], ws, 0).astype(vn.dtype)
    mixed = jnp.einsum('hts,bcshp->bcthp', ws_c, vp) + jnp.transpose(bs)[None, None, :, :, None]
    mixed = mixed.reshape(bsz, n_chunks * CHUNK, H, P)[:, :L]
    return u * mixed


def token_shift(p, prev_row, mu):
    prev = jnp.concatenate([prev_row[:, None].astype(p.dtype), p[:, :-1]], axis=1)
    return p + (prev - p) * mu, p[:, -1]


def wkv7_scan(r, decay, k, v, kk, a, S0):
    def step(S, inp):
        r_t, d_t, k_t, v_t, kk_t, a_t = inp
        sa = jnp.einsum('bhvk,bhk->bhv', S, -kk_t)
        S = S * d_t[:, :, None, :] + sa[..., None] * (kk_t * a_t)[:, :, None, :] + v_t[..., None] * k_t[:, :, None, :]
        y = jnp.einsum('bhvk,bhk->bhv', S, r_t)
        return S, y
    xs = tuple(jnp.swapaxes(t, 0, 1) for t in (r, decay, k, v, kk, a))
    S, ys = lax.scan(step, S0, xs)
    return jnp.swapaxes(ys, 0, 1), S


def parallel_mixer(h, shift0, wkv0, w_in, mu, gm_ln_g, gm_ln_b, gm_ws, gm_bs,
                   w0, w2, a0, a2, g2, k_k, k_a, r_k, ln_g, ln_b, w_out):
    bsz, L, _ = h.shape
    proj = h @ w_in
    z_a = jax.nn.gelu(proj[..., :2 * D_A], approximate=False)
    u = z_a[..., :D_A].reshape(bsz, L, N_HEADS_A, HEAD_A)
    vn = headnorm(z_a[..., D_A:].reshape(bsz, L, N_HEADS_A, HEAD_A), gm_ln_g, gm_ln_b, LN_EPS)
    out_a = chunk_spatial_gate(u, vn, gm_ws, gm_bs).reshape(bsz, L, D_A)
    chunk_start = ((L - 1) // CHUNK) * CHUNK
    chunk_v = vn[:, chunk_start:]
    pb, shift_new = token_shift(proj[..., 2 * D_A:], shift0, mu)
    pb = pb.astype(F32)
    offs = [D_B, 2 * D_B, 3 * D_B, 3 * D_B + D_DECAY_LORA, 3 * D_B + D_DECAY_LORA + D_AAA_LORA]
    r, k, v, wl, al, gl = jnp.split(pb, offs, axis=-1)
    heads = lambda t: t.reshape(bsz, L, N_HEADS_B, HEAD_B)
    w = -jax.nn.softplus(-(w0 + jnp.tanh(wl) @ w2)) - 0.5
    decay = jnp.exp(-jnp.exp(w))
    a = jax.nn.sigmoid(a0 + al @ a2)
    g = jax.nn.sigmoid(gl) @ g2
    kk = heads(k * k_k)
    kk = kk / jnp.maximum(jnp.sqrt(jnp.sum(kk * kk, axis=-1, keepdims=True)), 1e-12)
    k = k * (1.0 + (a - 1.0) * k_a)
    r_h, k_h, v_h = heads(r), heads(k), heads(v)
    yb, wkv_new = wkv7_scan(r_h, heads(decay), k_h, v_h, kk, heads(a), wkv0.astype(F32))
    yb = headnorm(yb, ln_g, ln_b, GN_EPS)
    bonus = jnp.sum(r_h * k_h * r_k.astype(F32), axis=-1, keepdims=True) * v_h
    out_b = ((yb + bonus).reshape(bsz, L, D_B) * g).astype(h.dtype)
    y = jnp.concatenate([out_a, out_b], axis=-1) @ w_out
    return y, chunk_v, shift_new, wkv_new


def memory_kv(mem, g, w_k, w_v):
    bsz = mem.shape[0]
    mn = rmsnorm(mem, g)
    mk = (mn @ w_k).reshape(bsz, N_MEM, N_HEADS_X, HEAD_X)
    mv = (mn @ w_v).reshape(bsz, N_MEM, N_HEADS_X, HEAD_X)
    return mk, mv


def cross_attn(h, mem_k, mem_v, w_q, w_o):
    bsz, L, _ = h.shape
    q = (h @ w_q).reshape(bsz, L, N_HEADS_X, HEAD_X)
    s = jnp.einsum('blhd,bmhd->bhlm', q.astype(F32), mem_k.astype(F32)) * (HEAD_X ** -0.5)
    p = jax.nn.softmax(s, axis=-1)
    o = jnp.einsum('bhlm,bmhd->blhd', p, mem_v.astype(F32)).astype(h.dtype)
    return o.reshape(bsz, L, D_MODEL) @ w_o


def sq_relu_ffn(h, w_up, w_down):
    return jnp.square(jax.nn.relu(h @ w_up)) @ w_down


def setup_inputs(seed: int = 0) -> dict:
    key = jax.random.key(seed)
    ks = iter(jax.random.split(key, 48))
    nrm = lambda shape, scale: scale * jax.random.normal(next(ks), shape, F32)
    L = DEPTH
    return {
        "x_prompt": nrm((BATCH, SEQ, D_MODEL), 1.0),
        "x_sample": nrm((DEC_BATCH, DEC_SEQ, D_MODEL), 1.0),
        "mem_prompt": nrm((BATCH, N_MEM, D_MODEL), 1.0),
        "cache_mem_k": nrm((L, DEC_BATCH, N_MEM, N_HEADS_X, HEAD_X), 1.0),
        "cache_mem_v": nrm((L, DEC_BATCH, N_MEM, N_HEADS_X, HEAD_X), 1.0),
        "state_shift": nrm((L, DEC_BATCH, D_SHIFT), 1.0),
        "state_wkv": nrm((L, DEC_BATCH, N_HEADS_B, HEAD_B, HEAD_B), 0.5),
        "norm_mix_g": 1.0 + nrm((L, D_MODEL), 0.1),
        "w_in": nrm((L, D_MODEL, D_IN), D_MODEL ** -0.5),
        "tshift_mu": jax.random.uniform(next(ks), (L, D_SHIFT), F32),
        "gm_ln_g": 1.0 + nrm((L, D_A), 0.1),
        "gm_ln_b": nrm((L, D_A), 0.01),
        "gm_ws": nrm((L, N_HEADS_A, CHUNK, CHUNK), CHUNK ** -0.5),
        "gm_bs": 1.0 + nrm((L, N_HEADS_A, CHUNK), 0.1),
        "rw_w0": -1.0 + nrm((L, D_B), 0.5),
        "rw_w2": nrm((L, D_DECAY_LORA, D_B), 0.1 * D_DECAY_LORA ** -0.5),
        "rw_a0": nrm((L, D_B), 0.1),
        "rw_a2": nrm((L, D_AAA_LORA, D_B), D_AAA_LORA ** -0.5),
        "rw_g2": nrm((L, D_GATE_LORA, D_B), D_GATE_LORA ** -0.5),
        "rw_kk": 0.85 + nrm((L, D_B), 0.05),
        "rw_ka": 1.0 + nrm((L, D_B), 0.05),
        "rw_rk": nrm((L, N_HEADS_B, HEAD_B), 0.1),
        "rw_ln_g": 1.0 + nrm((L, D_B), 0.1),
        "rw_ln_b": nrm((L, D_B), 0.01),
        "w_out": nrm((L, D_MODEL, D_MODEL), D_MODEL ** -0.5),
        "norm_x_g": 1.0 + nrm((L, D_MODEL), 0.1),
        "norm_mem_g": 1.0 + nrm((L, D_MODEL), 0.1),
        "w_xq": nrm((L, D_MODEL, D_MODEL), D_MODEL ** -0.5),
        "w_xk": nrm((L, D_MODEL, D_MODEL), D_MODEL ** -0.5),
        "w_xv": nrm((L, D_MODEL, D_MODEL), D_MODEL ** -0.5),
        "w_xo": nrm((L, D_MODEL, D_MODEL), D_MODEL ** -0.5),
        "norm_ffn_g": 1.0 + nrm((L, D_MODEL), 0.1),
        "w_up": nrm((L, D_MODEL, D_FF), D_MODEL ** -0.5),
        "w_down": nrm((L, D_FF, D_MODEL), D_FF ** -0.5),
        "final_g": 1.0 + nrm((D_MODEL,), 0.1),
    }


def reference(x_prompt, x_sample, mem_prompt, cache_mem_k, cache_mem_v, state_shift, state_wkv,
              norm_mix_g, w_in, tshift_mu, gm_ln_g, gm_ln_b, gm_ws, gm_bs,
              rw_w0, rw_w2, rw_a0, rw_a2, rw_g2, rw_kk, rw_ka, rw_rk, rw_ln_g, rw_ln_b, w_out,
              norm_x_g, norm_mem_g, w_xq, w_xk, w_xv, w_xo, norm_ffn_g, w_up, w_down, final_g):

    def layer(l, x, mem_k, mem_v, shift0, wkv0):
        y_mix, chunk_v, shift_new, wkv_new = parallel_mixer(
            rmsnorm(x, norm_mix_g[l]), shift0, wkv0, w_in[l], tshift_mu[l],
            gm_ln_g[l], gm_ln_b[l], gm_ws[l], gm_bs[l],
            rw_w0[l], rw_w2[l], rw_a0[l], rw_a2[l], rw_g2[l], rw_kk[l], rw_ka[l], rw_rk[l],
            rw_ln_g[l], rw_ln_b[l], w_out[l])
        x = x + y_mix
        x = x + cross_attn(rmsnorm(x, norm_x_g[l]), mem_k, mem_v, w_xq[l], w_xo[l])
        x = x + sq_relu_ffn(rmsnorm(x, norm_ffn_g[l]), w_up[l], w_down[l])
        return x, chunk_v, shift_new, wkv_new

    bp = x_prompt.shape[0]
    xp = x_prompt
    mk_p, mv_p, sh_p, wk_p, cv_p = [], [], [], [], []
    for l in range(DEPTH):
        mk, mv = memory_kv(mem_prompt, norm_mem_g[l], w_xk[l], w_xv[l])
        shift0 = jnp.zeros((bp, D_SHIFT), x_prompt.dtype)
        wkv0 = jnp.zeros((bp, N_HEADS_B, HEAD_B, HEAD_B), F32)
        xp, cv, sh, wk = layer(l, xp, mk, mv, shift0, wkv0)
        mk_p.append(mk); mv_p.append(mv); sh_p.append(sh); wk_p.append(wk); cv_p.append(cv)

    xs = x_sample
    sh_s, wk_s, cv_s = [], [], []
    for l in range(DEPTH):
        xs, cv, sh, wk = layer(l, xs, cache_mem_k[l], cache_mem_v[l], state_shift[l], state_wkv[l])
        sh_s.append(sh); wk_s.append(wk); cv_s.append(cv)

    y_prompt = rmsnorm(xp, final_g)
    y_sample = rmsnorm(xs, final_g)
    return (y_prompt, y_sample,
            jnp.stack(mk_p), jnp.stack(mv_p), jnp.stack(sh_p), jnp.stack(wk_p), jnp.stack(cv_p),
            jnp.stack(sh_s), jnp.stack(wk_s), jnp.stack(cv_s))
```

```python
import numpy as np
import concourse.bass as bass
import concourse.mybir as mybir
from concourse.bass_utils import run_bass_kernel_spmd

F32 = mybir.dt.float32
BF16 = mybir.dt.bfloat16
AF = mybir.ActivationFunctionType
ALU = mybir.AluOpType
AX = mybir.AxisListType


class Buf:
    def __init__(self, name):
        self.name = name
        self.last_w = None
        self.readers = []
        self.sem = None
        self.cnt = 0


class TV:
    def __init__(self, ap, buf):
        self.ap = ap
        self.buf = buf

    def __getitem__(self, key):
        return TV(self.ap[key], self.buf)

    def re(self, s, **kw):
        return TV(self.ap.rearrange(s, **kw), self.buf)

    def bc(self, shape):
        return TV(self.ap.to_broadcast(shape), self.buf)

    def un(self, ax):
        return TV(self.ap.unsqueeze(ax), self.buf)

    def bitcast(self, dt):
        return TV(self.ap.bitcast(dt), self.buf)

    def wb(self, buf):
        return TV(self.ap, buf)


class Op:
    __slots__ = ("eng", "fn", "reads", "writes", "dma", "deps", "sig", "ticket", "sem", "val", "idx")


ENGS = ("pe", "act", "dve", "pool", "sp")


class KB:
    def __init__(self):
        self.nc = bass.Bass("TRN2", target_bir_lowering=False)
        self.ops = []
        self.ctx = []
        self.stores = []
        self.dram_bufs = {}

    def enter(self, cm):
        v = cm.__enter__()
        self.ctx.append(cm)
        return v

    def sb(self, name, shape, dt=F32):
        t = self.enter(self.nc.sbuf_tensor(name, list(shape), dt))
        return TV(t[:], Buf(name))

    def ps(self, name, shape, dt=F32):
        t = self.enter(self.nc.psum_tensor(name, list(shape), dt))
        return TV(t[:], Buf(name))

    def dram(self, name, shape, dt=F32, kind="ExternalInput"):
        t = self.nc.dram_tensor(name, list(shape), dt, kind=kind)
        return TV(t.ap(), None)

    def arena_init(self, nbytes):
        self.arena = self.enter(self.nc.sbuf_tensor("arena", [128, nbytes // 4], F32))
        self.arena_size = nbytes
        self.aoff = 0
        self.phase = None
        self.amax = 0
        self._bar_ap = self.alloc("bar", [128, 8]).ap
        self.phase = Buf("phase")

    def alloc(self, name, shape, dt=F32):
        esz = 2 if dt == BF16 else 4
        n = esz
        for s in shape[1:]:
            n *= s
        n = (n + 31) // 32 * 32
        assert self.aoff + n <= self.arena_size, "arena overflow at %s: %d + %d > %d" % (name, self.aoff, n, self.arena_size)
        ap = self.arena[:, self.aoff // 4:(self.aoff + n) // 4]
        raw = ap
        self.aoff += n
        self.amax = max(self.amax, self.aoff)
        if dt == BF16:
            ap = ap.bitcast(BF16)
        tot = 1
        for s in shape[1:]:
            tot *= s
        ap = ap[:, 0:tot]
        if len(shape) == 3:
            ap = ap.rearrange("p (a b) -> p a b", a=shape[1])
        elif len(shape) == 4:
            ap = ap.rearrange("p (a b c) -> p a b c", a=shape[1], b=shape[2])
        elif len(shape) == 5:
            ap = ap.rearrange("p (a b c d) -> p a b c d", a=shape[1], b=shape[2], c=shape[3])
        if shape[0] < 128:
            ap = ap[0:shape[0]]
        tv = TV(ap, Buf(name))
        tv.raw = raw
        return tv

    def alias(self, tv, shape, dt=F32, off=0):
        esz = 2 if dt == BF16 else 4
        tot = 1
        for s_ in shape[1:]:
            tot *= s_
        nw = (tot * esz + 3) // 4
        ap = tv.raw[:, off:off + nw]
        assert off + nw <= tv.raw.shape[1], "alias too large"
        if dt == BF16:
            ap = ap.bitcast(BF16)
        ap = ap[:, 0:tot]
        if len(shape) == 3:
            ap = ap.rearrange("p (a b) -> p a b", a=shape[1])
        elif len(shape) == 4:
            ap = ap.rearrange("p (a b c) -> p a b c", a=shape[1], b=shape[2])
        if shape[0] < 128:
            ap = ap[0:shape[0]]
        return TV(ap, tv.buf)

    def barrier(self):
        o = self.op("dve", lambda e: e.memset(self._bar_ap, 0.0), [], [])
        o.writes = [self.phase]

    def op(self, eng, fn, reads, writes, dma=False):
        o = Op()
        o.eng = eng
        o.fn = fn
        o.reads = [t.buf for t in reads if isinstance(t, TV) and t.buf is not None]
        o.writes = [t.buf for t in writes if isinstance(t, TV) and t.buf is not None]
        if getattr(self, "phase", None) is not None:
            o.reads.append(self.phase)
        o.dma = dma
        o.idx = len(self.ops)
        self.ops.append(o)
        return o

    @staticmethod
    def _a(x):
        return x.ap if isinstance(x, TV) else x

    def mm(self, out, lhsT, rhs, start=True, stop=True):
        a = self._a
        self.op("pe", lambda e: e.matmul(a(out), lhsT=a(lhsT), rhs=a(rhs), start=start, stop=stop), [lhsT, rhs], [out])

    def tr(self, out, in_, ident):
        a = self._a
        self.op("pe", lambda e: e.transpose(out=a(out), in_=a(in_), identity=a(ident)), [in_, ident], [out])

    def act(self, out, in_, func, bias=None, scale=None, accum=None, eng="act"):
        a = self._a
        kw = {}
        if bias is not None:
            kw["bias"] = a(bias)
        if scale is not None:
            kw["scale"] = a(scale)
        if accum is not None:
            kw["accum_out"] = a(accum)
        self.op(eng, lambda e: e.activation(out=a(out), in_=a(in_), func=func, **kw), [in_, bias, scale], [out, accum])

    def tt(self, eng, out, in0, in1, op):
        a = self._a
        self.op(eng, lambda e: e.tensor_tensor(out=a(out), in0=a(in0), in1=a(in1), op=op), [in0, in1], [out])

    def ts(self, eng, out, in0, s1, s2=None, op0=ALU.mult, op1=None, accum=None):
        a = self._a
        kw = {}
        if op1 is not None:
            kw["op1"] = op1
        if accum is not None:
            kw["accum_out"] = a(accum)
        self.op(eng, lambda e: e.tensor_scalar(out=a(out), in0=a(in0), scalar1=a(s1), scalar2=a(s2), op0=op0, **kw),
                [in0, s1, s2], [out, accum])

    def stt(self, eng, out, in0, scalar, in1, op0, op1):
        a = self._a
        self.op(eng, lambda e: e.scalar_tensor_tensor(out=a(out), in0=a(in0), scalar=a(scalar), in1=a(in1), op0=op0, op1=op1),
                [in0, scalar, in1], [out])

    def copy(self, eng, out, in_):
        a = self._a
        if eng == "act":
            self.op(eng, lambda e: e.copy(out=a(out), in_=a(in_)), [in_], [out])
        else:
            self.op(eng, lambda e: e.tensor_copy(out=a(out), in_=a(in_)), [in_], [out])

    def red(self, eng, out, in_, op=ALU.add, axis=AX.X):
        a = self._a
        self.op(eng, lambda e: e.tensor_reduce(out=a(out), in_=a(in_), axis=axis, op=op), [in_], [out])

    def recip(self, out, in_):
        a = self._a
        self.op("dve", lambda e: e.reciprocal(out=a(out), in_=a(in_)), [in_], [out])

    def memset(self, eng, out, val):
        a = self._a
        self.op(eng, lambda e: e.memset(a(out), val), [], [out])

    def scan(self, out, d0, d1, init=0.0, op0=ALU.mult, op1=ALU.add):
        a = self._a
        self.op("dve", lambda e: e.tensor_tensor_scan(out=a(out), data0=a(d0), data1=a(d1), initial=init, op0=op0, op1=op1),
                [d0, d1], [out])

    def aselect(self, out, in_, pattern, cmp, fill, base=0, cm=0):
        a = self._a
        self.op("pool", lambda e: e.affine_select(out=a(out), in_=a(in_), pattern=pattern, compare_op=cmp, fill=fill,
                                                   base=base, channel_multiplier=cm), [in_], [out])

    def dma(self, q, out, in_, noncontig=False):
        a = self._a
        nc = self.nc

        def fn(e):
            if noncontig:
                with nc.allow_non_contiguous_dma(reason="small strided"):
                    return e.dma_start(out=a(out), in_=a(in_))
            return e.dma_start(out=a(out), in_=a(in_))

        o = self.op(q, fn, [in_], [out], dma=True)
        return o

    def finish(self):
        nc = self.nc
        ops = self.ops
        for o in ops:
            deps = set()
            for b in o.reads:
                if b.last_w is not None:
                    deps.add(b.last_w)
            for b in o.writes:
                if b.last_w is not None:
                    deps.add(b.last_w)
                deps.update(b.readers)
            deps.discard(o)
            o.deps = deps
            for b in o.reads:
                b.readers.append(o)
            for b in o.writes:
                b.last_w = o
                b.readers = []
            if o.dma:
                cand = [b for b in (o.writes + o.reads) if b is not self.phase]
                sb = cand[0]
                o.sem = sb
                sb.cnt += 16
                o.val = sb.cnt
        for o in ops:
            nd = set()
            for d in o.deps:
                if o.eng == "pe" and (not o.dma) and d.eng == "pe" and (not d.dma):
                    continue
                if o.dma and d.dma and o.sem is d.sem:
                    continue
                nd.add(d)
            o.deps = nd
        for o in ops:
            o.sig = False
        for o in ops:
            for d in o.deps:
                d.sig = True
        cnt = {e: 0 for e in ENGS}
        for o in ops:
            if o.dma:
                continue
            if o.sig:
                cnt[o.eng] += 1
                o.ticket = cnt[o.eng]
        esem = {e: self.enter(nc.semaphore("es_" + e)) for e in ENGS}
        dsems = {}
        for o in ops:
            if o.dma and o.sem not in dsems:
                dsems[o.sem] = self.enter(nc.semaphore("ds_%d" % len(dsems)))
        self.n_dsems = len(dsems)
        final_waits = {}
        for o in ops:
            if o.dma:
                final_waits[dsems[o.sem]] = max(final_waits.get(dsems[o.sem], 0), o.val)

        per = {e: [o for o in ops if o.eng == e] for e in ENGS}

        def emit(ename, e):
            waited = {}
            for o in per[ename]:
                need = {}
                for d in o.deps:
                    if d.dma:
                        s, v = dsems[d.sem], d.val
                    else:
                        s, v = esem[d.eng], d.ticket
                    if need.get(s, 0) < v:
                        need[s] = v
                for s, v in need.items():
                    if waited.get(s, 0) < v:
                        e.wait_ge(s, v)
                        waited[s] = v
                ins = o.fn(e)
                if o.dma:
                    ins.then_inc(dsems[o.sem], 16)
                elif o.sig:
                    ins.then_inc(esem[ename], 1)
            if ename == "sp":
                for s, v in final_waits.items():
                    e.wait_ge(s, v)

        block = self.enter(nc.Block())

        @block.tensor
        def _(e):
            emit("pe", e)

        @block.scalar
        def _(e):
            emit("act", e)

        @block.vector
        def _(e):
            emit("dve", e)

        @block.gpsimd
        def _(e):
            emit("pool", e)

        @block.sync
        def _(e):
            emit("sp", e)

        for cm in reversed(self.ctx):
            cm.__exit__(None, None, None)
        self.ctx = []
        return nc

import math
import os
CUT = int(os.environ.get('KCUT', '0'))

P = 128
TT = 128
NT = int(os.environ.get('KNT', '16'))
NCH = TT // 128
NSB = 16
EPS = 1e-6
W_NAMES = ["norm_mix_g", "w_in", "tshift_mu", "gm_ln_g", "gm_ln_b", "gm_ws", "gm_bs", "rw_w0", "rw_w2", "rw_a0",
           "rw_a2", "rw_g2", "rw_kk", "rw_ka", "rw_rk", "rw_ln_g", "rw_ln_b", "w_out", "norm_x_g", "norm_mem_g",
           "w_xq", "w_xk", "w_xv", "w_xo", "norm_ffn_g", "w_up", "w_down", "final_g"]
W_SHAPES = {"norm_mix_g": [1024], "w_in": [1024, 2816], "tshift_mu": [1792], "gm_ln_g": [512], "gm_ln_b": [512],
            "gm_ws": [4, 128, 128], "gm_bs": [4, 128], "rw_w0": [512], "rw_w2": [64, 512], "rw_a0": [512],
            "rw_a2": [64, 512], "rw_g2": [128, 512], "rw_kk": [512], "rw_ka": [512], "rw_rk": [512],
            "rw_ln_g": [512], "rw_ln_b": [512], "w_out": [1024, 1024], "norm_x_g": [1024], "norm_mem_g": [1024],
            "w_xq": [1024, 1024], "w_xk": [1024, 1024], "w_xv": [1024, 1024], "w_xo": [1024, 1024],
            "norm_ffn_g": [1024], "w_up": [1024, 4096], "w_down": [4096, 1024], "final_g": [1024]}
OUT_SHAPES = {"y_p": [2048, 1024], "y_s": [16, 1024], "mk_o": [256, 1024], "mv_o": [256, 1024], "shp_o": [1792],
              "wkvp_o": [8, 64, 64], "cvp_o": [128, 512], "shs_o": [16, 1792], "wkvs_o": [16, 8, 64, 64],
              "cvs_o": [16, 512]}


def build(taps=(), stages=99, arena_kb=207):
    k = KB()
    nc = k.nc
    D = {}
    D["xp"] = k.dram("xp", [2048, 1024])
    D["xs"] = k.dram("xs", [16, 1024])
    D["mem"] = k.dram("mem", [256, 1024])
    D["ck"] = k.dram("ck", [16, 256, 1024])
    D["cv"] = k.dram("cv", [16, 256, 1024])
    D["ssh"] = k.dram("ssh", [16, 1792])
    D["swkv"] = k.dram("swkv", [16, 8, 64, 64])
    for n in W_NAMES:
        D[n] = k.dram(n, W_SHAPES[n])
    O = {n: k.dram(n, s, kind="ExternalOutput") for n, s in OUT_SHAPES.items()}
    tapd = {}

    pb = [k.ps("pb%d" % i, [128, 512]) for i in range(8)]
    k.arena_init(arena_kb * 1024)
    A = k.alloc

    def tap(name, tv, shape):
        if name not in taps:
            return
        t = k.dram("tap_" + name, list(shape), dt=tv.ap.dtype, kind="ExternalOutput")
        tapd[name] = t
        k.dma("sp", t, tv)

    big_i = [0]

    def nextbig():
        big_i[0] += 1
        return pb[6 + big_i[0] % 2]

    ident = A("ident", [128, 128], BF16)
    identf = A("identf", [128, 128], F32)
    for t in (ident, identf):
        k.memset("pool", t, 0.0)
        k.aselect(t, t, [[-1, 128]], ALU.not_equal, 1.0, base=0, cm=1)
    mask2 = A("mask2", [128, 2, 128], F32)
    k.memset("pool", mask2, 1.0)
    k.aselect(mask2[:, 0, :], mask2[:, 0, :], [[1, 128]], ALU.is_gt, 0.0, base=0, cm=-1)
    k.aselect(mask2[:, 1, :], mask2[:, 1, :], [[1, 128]], ALU.is_ge, 0.0, base=0, cm=-1)
    maskN = A("maskN", [128, 128], F32)
    k.memset("pool", maskN, 1.0)
    k.aselect(maskN, maskN, [[-1, 128]], ALU.is_gt, 0.0, base=0, cm=1)
    segm = A("segm", [128, TT], F32)
    k.memset("pool", segm, 1.0)
    k.memset("pool", segm.re("p (j t) -> p j t", t=128)[:, :, 0:1], 0.0)
    blk2 = A("blk2", [128, 128], F32)
    k.memset("pool", blk2, 1.0)
    k.memset("pool", blk2[0:64, 64:128], 0.0)
    k.memset("pool", blk2[64:128, 0:64], 0.0)
    hsel = A("hsel", [128, 2], BF16)
    k.memset("pool", hsel, 0.0)
    k.memset("pool", hsel[0:64, 0:1], 1.0)
    k.memset("pool", hsel[64:128, 1:2], 1.0)

    prm = A("prm", [128, 80], F32)
    G_MIX, G_X, G_FFN, G_MEM = 0, 1, 2, 3
    for i, n in enumerate(["norm_mix_g", "norm_x_g", "norm_ffn_g", "norm_mem_g"]):
        k.dma("sp", prm[:, i * 8:(i + 1) * 8], D[n].re("(c p) -> p c", p=128), noncontig=True)
    k.dma("sp", prm[:, 40:54], D["tshift_mu"].re("(c p) -> p c", p=128), noncontig=True)
    for i, n in enumerate(["rw_w0", "rw_a0", "rw_kk", "rw_ka", "rw_rk"]):
        k.dma("sp", prm[:, 54 + 4 * i:58 + 4 * i], D[n].re("(c p) -> p c", p=128), noncontig=True)
    gT = lambda gi: prm[:, gi * 8:(gi + 1) * 8]
    mu = prm[:, 40:54]
    pw0, pa0, pkk, pka, prk = (prm[:, 54 + 4 * i:58 + 4 * i] for i in range(5))
    omm = A("omm", [128, 14], F32)
    k.ts("dve", omm, mu, -1.0, 1.0, op0=ALU.mult, op1=ALU.add)

    bct = A("bct", [128, 5, 512], F32)
    for i, n in enumerate(["gm_ln_g", "gm_ln_b", "rw_ln_g", "rw_ln_b"]):
        k.dma("sp", bct[:, i, :], TV(D[n].ap.partition_broadcast(128), None))
    k.dma("sp", bct[:, 4, :], TV(D["gm_bs"].ap.rearrange("h t -> (h t)").partition_broadcast(128), None))

    w2a2 = A("w2a2", [128, 512], BF16)
    k.dma("pool", w2a2[0:64, :], D["rw_w2"])
    k.dma("pool", w2a2[64:128, :], D["rw_a2"])
    g2 = A("g2", [128, 512], BF16)
    k.dma("pool", g2, D["rw_g2"])

    xr = [A("xr%d" % j, [128, 1024], F32) for j in range(16)]
    xpv = D["xp"].re("(j p) d -> j p d", p=128)
    for j in range(2):
        k.dma("sp", xr[j], xpv[j])

    xsb = [A("xsb%d" % i, [128, 1024], BF16) for i in range(1)]
    sqs = A("sqs", [128, 512], F32)
    junk = TV(sqs.ap.bitcast(BF16), sqs.buf)

    ss = A("ss", [128, 8], F32)
    rs = A("rs", [128, 8], F32)
    lnst = A("lnst", [128, 48], F32)

    def rms_A(xtvs, pt, xbs, rs_, junk_=None, ss_=None):
        n = len(xtvs)
        junk_ = junk if junk_ is None else junk_
        ss_ = ss if ss_ is None else ss_
        k.memset("dve", ss_, 0.0)
        for j, x in enumerate(xtvs):
            k.act(junk_[:pt], x, AF.Square, accum=ss_[:pt, j:j + 1])
        k.act(rs_[:pt, 0:n], ss_[:pt, 0:n], AF.Sqrt, bias=EPS, scale=1.0 / 1024)
        k.recip(rs_[:pt, 0:n], rs_[:pt, 0:n])
        for j, x in enumerate(xtvs):
            k.act(xbs[j][:pt], x, AF.Copy, scale=rs_[:pt, j:j + 1])

    def rms_B(n, pt, gi, hT, toff, xbs):
        for j in range(n):
            ptv = nextbig().bitcast(BF16).re("p (c t) -> p c t", c=8)
            for c in range(8):
                k.tr(ptv[:, c, 0:pt], xbs[j][:pt, c * 128:(c + 1) * 128], ident[:pt, :pt])
            k.tt("dve", hT[:, :, toff + j * pt:toff + (j + 1) * pt], ptv[:, :, 0:pt],
                 gT(gi).un(2).bc([128, 8, pt]), ALU.mult)

    def rms_T(xtvs, pt, gi, hT, toff):
        n = len(xtvs)
        k.memset("dve", ss, 0.0)
        for j, x in enumerate(xtvs):
            k.act(junk[:pt], x, AF.Square, accum=ss[:pt, j:j + 1])
        k.act(rs[:pt, 0:n], ss[:pt, 0:n], AF.Sqrt, bias=EPS, scale=1.0 / 1024)
        k.recip(rs[:pt, 0:n], rs[:pt, 0:n])
        for j, x in enumerate(xtvs):
            xb = xsb[0]
            k.act(xb[:pt], x, AF.Copy, scale=rs[:pt, j:j + 1])
            ptv = nextbig().bitcast(BF16).re("p (c t) -> p c t", c=8)
            for c in range(8):
                k.tr(ptv[:, c, 0:pt], xb[:pt, c * 128:(c + 1) * 128], ident[:pt, :pt])
            k.tt("dve", hT[:, :, toff + j * pt:toff + (j + 1) * pt], ptv[:, :, 0:pt],
                 gT(gi).un(2).bc([128, 8, pt]), ALU.mult)
        return rs

    def ln_heads(pt, x, H, Dh, eps, gb, bb, out, scr):
        x3 = x.re("p (h d) -> p h d", h=H)
        s3 = scr.re("p (h d) -> p h d", h=H)
        st = lnst
        s1, s2, mean, msq, var = (st[:pt, i * 8:i * 8 + H] for i in range(5))
        k.red("dve", s1, x3[:pt])
        k.act(scr[:pt], x[:pt], AF.Square)
        k.red("dve", s2, s3[:pt])
        k.ts("dve", mean, s1, 1.0 / Dh, None, op0=ALU.mult)
        k.tt("dve", msq, mean, mean, ALU.mult)
        k.stt("dve", var, s2, 1.0 / Dh, msq, op0=ALU.mult, op1=ALU.subtract)
        k.act(var, var, AF.Sqrt, bias=eps, scale=1.0)
        k.recip(var, var)
        k.tt("dve", s3[:pt], x3[:pt], mean.un(2).bc([pt, H, Dh]), ALU.subtract)
        k.tt("dve", s3[:pt], s3[:pt], var.un(2).bc([pt, H, Dh]), ALU.mult)
        k.tt("dve", scr[:pt], scr[:pt], gb[:pt], ALU.mult)
        k.tt("dve", out[:pt], scr[:pt], bb[:pt], ALU.add)


    B = NSB
    xsr = A("xsr", [B, 1024], F32)
    k.dma("sp", xsr, D["xs"])

    def make_sel(dt=F32):
        sel_ = A("sel", [B, B, 128], dt)
        k.memset("pool", sel_, 1.0)
        k.aselect(sel_, sel_, [[-1, B], [0, 128]], ALU.is_equal, 0.0, base=0, cm=1)
        return sel_

    def sample_S1():
        mixTs = A("mixTs", [128, 8, B], BF16)
        nkk = A("s_kk", [B, 512], F32)
        dcy = A("s_d", [B, 512], F32)
        bvec = A("s_b", [B, 512], F32)
        kmods = A("s_k", [B, 512], F32)
        rtok = A("s_r", [B, 512], F32)
        vtok = A("s_v", [B, 512], F32)
        gtok = A("s_g", [B, 512], F32)
        bons = A("s_bon", [B, 8], F32)
        mS1x = k.aoff
        bcs = [A("bcs%d" % i, [B, 512], F32) for i in range(3)]

        def bload(slot, n):
            k.dma("sp", bcs[slot], TV(D[n].ap.partition_broadcast(B), None))
            return bcs[slot]
        w0b = bload(0, "rw_w0")
        a0b = bload(1, "rw_a0")
        kkb = bload(2, "rw_kk")
        hTs = A("hTs", [128, 8, B], BF16)
        ptok = A("ptok", [B, 2816], F32)
        sshb = [A("sshb%d" % i, [B, 512], F32) for i in range(2)]
        mubb = [A("mubb%d" % i, [B, 512], F32) for i in range(2)]
        ws0 = A("ws0", [B, 8], F32)
        k.dma("sp", ws0[:, 0:4], TV(D["gm_ws"].ap.rearrange("h t s -> h (t s)")[:, 0].partition_broadcast(B), None), noncontig=True)
        k.dma("sp", ws0[:, 4:8], TV(D["gm_bs"].ap[:, 0].partition_broadcast(B), None), noncontig=True)
        pbs = A("pbs", [B, 1792], F32)
        vns = A("vns", [B, 512], F32)
        oas = A("oas", [B, 512], BF16)
        tws = A("tws", [B, 256], BF16)
        twT = A("twT", [128, 2, B], BF16)
        s1 = A("s_t1", [B, 512], F32)
        s2 = A("s_t2", [B, 512], F32)
        s8 = A("s_s8", [B, 16], F32)

        rms_T([xsr], B, G_MIX, hTs, 0)
        for c0 in range(0, 2816, 512):
            n = min(512, 2816 - c0)
            ps = nextbig()
            for c in range(8):
                k.mm(ps[:B, 0:n], hTs[:, c, :], win[:, c, c0:c0 + n], start=(c == 0), stop=(c == 7))
            k.act(ptok[:, c0:c0 + n], ps[:B, 0:n], AF.Gelu if c0 < 1024 else AF.Copy)
        k.dma("sp", O["shs_o"], ptok[:, 1024:2816])
        ln_heads(B, ptok[:, 512:1024], 4, 128, 1e-5, bct[:, 0, :], bct[:, 1, :], vns, sqs)
        k.dma("sp", O["cvs_o"], vns)
        k.tt("pool", s1.re("p (h d) -> p h d", h=4), vns.re("p (h d) -> p h d", h=4), ws0[:, 0:4].un(2).bc([B, 4, 128]), ALU.mult)
        k.tt("pool", s1.re("p (h d) -> p h d", h=4), s1.re("p (h d) -> p h d", h=4), ws0[:, 4:8].un(2).bc([B, 4, 128]), ALU.add)
        k.tt("pool", oas, s1, ptok[:, 0:512], ALU.mult)
        ptv = nextbig().bitcast(BF16).re("p (c t) -> p c t", c=8)
        for h in range(4):
            k.tr(ptv[:, h, 0:B], oas[:, h * 128:(h + 1) * 128], ident[:B, :B])
        k.copy("dve", mixTs[:, 0:4, :], ptv[:, 0:4, 0:B])
        for bi, c0 in enumerate(range(0, 1792, 512)):
            n = min(512, 1792 - c0)
            sb_, mb_ = sshb[bi % 2][:, 0:n], mubb[bi % 2][:, 0:n]
            k.dma("sp", sb_, D["ssh"][:, c0:c0 + n])
            k.dma("sp", mb_, TV(D["tshift_mu"].ap[c0:c0 + n].partition_broadcast(B), None))
            pp = ptok[:, 1024 + c0:1024 + c0 + n]
            k.tt("dve", pbs[:, c0:c0 + n], sb_, pp, ALU.subtract)
            k.tt("dve", pbs[:, c0:c0 + n], pbs[:, c0:c0 + n], mb_, ALU.mult)
            k.tt("dve", pbs[:, c0:c0 + n], pbs[:, c0:c0 + n], pp, ALU.add)
        tap("s_pbs", pbs, [B, 1792])
        r_, k_, v_ = pbs[:, 0:512], pbs[:, 512:1024], pbs[:, 1024:1536]
        k.copy("dve", rtok, r_)
        k.copy("dve", vtok, v_)
        k.act(tws[:, 0:64], pbs[:, 1536:1600], AF.Tanh)
        k.copy("dve", tws[:, 64:128], pbs[:, 1600:1664])
        k.act(tws[:, 128:256], pbs[:, 1664:1792], AF.Sigmoid)
        ptv = nextbig().bitcast(BF16).re("p (c t) -> p c t", c=8)
        for i in range(2):
            k.tr(ptv[:, i, 0:B], tws[:, i * 128:(i + 1) * 128], ident[:B, :B])
        k.copy("dve", twT, ptv[:, 0:2, 0:B])
        ps = nextbig()
        k.mm(ps[:B, :], twT[0:64, 0, :], w2a2[0:64, :])
        k.tt("dve", s1, ps[:B, :], w0b, ALU.add)
        k.act(s1, s1, AF.Sigmoid)
        k.ts("dve", s1, s1, -math.exp(-0.5), None, op0=ALU.mult)
        k.act(dcy, s1, AF.Exp)
        ps = nextbig()
        k.mm(ps[:B, :], twT[64:128, 0, :], w2a2[64:128, :])
        k.tt("dve", s2, ps[:B, :], a0b, ALU.add)
        k.act(s2, s2, AF.Sigmoid)
        ps = nextbig()
        k.mm(ps[:B, :], twT[:, 1, :], g2)
        k.copy("dve", gtok, ps[:B, :])
        k.tt("pool", nkk, k_, kkb, ALU.mult)
        k.tt("pool", s1, nkk, nkk, ALU.mult)
        k.red("dve", s8[:, 0:8], s1.re("p (h d) -> p h d", h=8))
        k.act(s8[:, 0:8], s8[:, 0:8], AF.Sqrt)
        k.ts("dve", s8[:, 0:8], s8[:, 0:8], 1e-12, None, op0=ALU.max)
        k.recip(s8[:, 0:8], s8[:, 0:8])
        k.tt("pool", nkk.re("p (h d) -> p h d", h=8), nkk.re("p (h d) -> p h d", h=8), s8[:, 0:8].un(2).bc([B, 8, 64]), ALU.mult)
        k.tt("pool", bvec, nkk, s2, ALU.mult)
        kab = bload(0, "rw_ka")
        rkb = bload(1, "rw_rk")
        k.ts("dve", s1, s2, -1.0, None, op0=ALU.add)
        k.tt("dve", s1, s1, kab, ALU.mult)
        k.ts("dve", s1, s1, 1.0, None, op0=ALU.add)
        k.tt("dve", kmods, k_, s1, ALU.mult)
        k.tt("dve", s1, r_, kmods, ALU.mult)
        k.tt("dve", s1, s1, rkb, ALU.mult)
        k.red("dve", bons, s1.re("p (h d) -> p h d", h=8))
        tap("s_kk", nkk, [B, 512])
        tap("s_d", dcy, [B, 512])
        k.barrier()
        k.aoff = mS1x
        wq_pf = k.alias(win, [128, 8, 1024], BF16, off=0)
        wo_pf = k.alias(win, [128, 8, 1024], BF16, off=4096)
        for c in range(8):
            k.dma("pool", wq_pf[:, c, :], D["w_xq"].re("(c p) n -> c p n", p=128)[c])
        for c in range(8):
            k.dma("pool", wo_pf[:, c, :], D["w_xo"].re("(c p) n -> c p n", p=128)[c])
        sel = make_sel(BF16)
        St = [A("St%d" % i, [64, 8, 64], F32) for i in range(2)]
        Sn = [A("Sn%d" % i, [64, 8, 64], F32) for i in range(2)]
        tS = A("tS", [64, 8, 64], F32)
        vTs = A("vTs", [64, B, 8], F32)
        yTs = A("yTs", [64, B, 8], F32)
        sa = A("sa", [64, 8], F32)
        Ys = A("Ys", [B, 512], F32)
        yns = A("yns", [B, 512], F32)
        obs = A("obs", [B, 512], BF16)
        psv = [pb[0], pb[1]]
        for h in range(8):
            k.tr(psv[h % 2][0:64, (h // 2) * B:(h // 2 + 1) * B], vtok[:, h * 64:(h + 1) * 64], identf[:B, :B])
        for e in range(2):
            k.copy("dve", vTs.re("p b h -> p h b")[:, e::2, :], psv[e][0:64, 0:4 * B].re("p (q b) -> p q b", q=4))
        swv = D["swkv"].re("b h v k -> b v h k")
        wsv = O["wkvs_o"].re("b h v k -> b v h k")
        vecb = A("vecb", [B, 5, 512], BF16)
        for i, vec in enumerate((nkk, dcy, bvec, kmods, rtok)):
            k.copy("dve", vecb[:, i, :], vec)
        vecs = [vecb[:, i, :] for i in range(5)]
        bcb = [[A("bcb%d_%d" % (j, i), [64, 512], F32) for i in range(5)] for j in range(1)]
        bcb2 = [A("bcbx_%d" % i, [64, 512], F32) for i in range(2, 5)]
        tS2 = A("tS2", [64, 8, 64], F32)
        for b in range(B):
            st, sn = St[b % 2], Sn[b % 2]
            k.dma("sp", st, swv[b])
            pv_ = [pb[i] for i in range(5)]
            bc_ = bcb[0] if b % 2 == 0 else bcb[0][0:2] + bcb2
            for i, vec in enumerate(vecs):
                k.mm(pv_[i][0:64, :], sel[:, b, 0:64], vec)
                k.act(bc_[i], pv_[i][0:64, :], AF.Copy)
            f2 = lambda t: t.re("p h k -> p (h k)")
            r3 = lambda t: t.re("p (h k) -> p h k", h=8)
            k.tt("dve", f2(tS), f2(st), bc_[0], ALU.mult)
            k.red("dve", sa, tS)
            k.tt("pool", f2(sn), f2(st), bc_[1], ALU.mult)
            k.tt("dve", tS2, r3(bc_[2]), sa.un(2).bc([64, 8, 64]), ALU.mult)
            k.tt("pool", sn, sn, tS2, ALU.subtract)
            k.tt("dve", tS, r3(bc_[3]), vTs[:, b, :].un(2).bc([64, 8, 64]), ALU.mult)
            k.tt("pool", sn, sn, tS, ALU.add)
            k.dma("pool", wsv[b], sn)
            k.tt("dve", f2(tS2), f2(sn), bc_[4], ALU.mult)
            k.red("dve", yTs[:, b, :], tS2)
        psy = [pb[5], pb[6]]
        for h in range(8):
            k.tr(psy[h % 2][:B, (h // 2) * 64:(h // 2 + 1) * 64], yTs[:, :, h], identf[0:64, 0:64])
        for e in range(2):
            k.copy("dve", Ys.re("p (q e v) -> p q e v", q=4, e=2)[:, :, e, :], psy[e][:B, 0:256].re("p (q v) -> p q v", q=4))
        tap("s_Y", Ys, [B, 512])
        ln_heads(B, Ys, 8, 64, 64e-5, bct[:, 2, :], bct[:, 3, :], yns, sqs)
        k.tt("dve", sqs[:B].re("p (h v) -> p h v", h=8), vtok.re("p (h v) -> p h v", h=8), bons.un(2).bc([B, 8, 64]), ALU.mult)
        k.tt("dve", yns, yns, sqs[:B], ALU.add)
        k.tt("dve", obs, yns, gtok, ALU.mult)
        ptv = nextbig().bitcast(BF16).re("p (c t) -> p c t", c=8)
        for p in range(4):
            k.tr(ptv[:, p, 0:B], obs[:, p * 128:(p + 1) * 128], ident[:B, :B])
        k.copy("dve", mixTs[:, 4:8, :], ptv[:, 0:4, 0:B])
        for hf in range(2):
            ps = nextbig()
            for c in range(8):
                k.mm(ps[:B, :], mixTs[:, c, :], wout[:, c, hf * 512:(hf + 1) * 512], start=(c == 0), stop=(c == 7))
            xs_ = xsr[:, hf * 512:(hf + 1) * 512]
            k.tt("dve", xs_, ps[:B, :], xs_, ALU.add)
        tap("s_x1", xsr, [B, 1024])


    def sample_S2():
        sel = make_sel(BF16)
        hT2 = A("hTs2", [128, 8, B], BF16)
        qtok = A("qtok", [B, 1024], BF16)
        KV = [A("KV%d" % i, [128, 2, 1024], F32) for i in range(5)]
        sall = A("sall", [128, 2, B, 4], F32)
        prod = [A("prod%d" % i, [128, 512], F32) for i in range(2)]
        qbs = [A("qbs%d" % i, [128, 1024], F32) for i in range(2)]
        Ps = A("Ps", [64, 256], F32)
        PTs = A("PTs", [128, 2, 64], BF16)
        vbf = [A("vbf%d" % i, [128, 2, 1024], BF16) for i in range(2)]
        oTs = A("oTs", [128, 8, B], BF16)
        sm2 = A("sm2", [64, 8], F32)
        rms_T([xsr], B, G_X, hT2, 0)
        for hf in range(2):
            ps = nextbig()
            for c in range(8):
                k.mm(ps[:B, :], hT2[:, c, :], wq[:, c, hf * 512:(hf + 1) * 512], start=(c == 0), stop=(c == 7))
            k.ts("dve", qtok[:, hf * 512:(hf + 1) * 512], ps[:B, :], 0.0625, None, op0=ALU.mult)
        ckv = D["ck"].re("b (mt m) d -> b m mt d", mt=2)
        cvv = D["cv"].re("b (mt m) d -> b m mt d", mt=2)
        for b in range(B):
            kt = KV[b % 5]
            k.dma("sp", kt, ckv[b])
            qb = [pb[0], pb[1]]
            for hf in range(2):
                k.mm(qb[hf], sel[:, b, :], qtok[:, hf * 512:(hf + 1) * 512])
                k.act(qbs[b % 2][:, hf * 512:(hf + 1) * 512], qb[hf], AF.Copy)
            for mt in range(2):
                for hf in range(2):
                    pr = prod[(mt * 2 + hf) % 2]
                    k.tt("pool" if (mt * 2 + hf) % 2 == 0 else "dve", pr, kt[:, mt, hf * 512:(hf + 1) * 512], qbs[b % 2][:, hf * 512:(hf + 1) * 512], ALU.mult)
                    k.red("dve", sall[:, mt, b, 2 * hf:2 * hf + 2], pr.re("p (h d) -> p h d", h=2))
        pst = pb[2]
        for mt in range(2):
            k.tr(pst[0:64, mt * 128:(mt + 1) * 128], sall[:, mt].re("p b h -> p (b h)"), identf)
        mx, nmx, ssum, rinv = (sm2[:, i:i + 1] for i in range(4))
        k.red("dve", mx, pst[0:64, 0:256], op=ALU.max)
        k.ts("dve", nmx, mx, -1.0, None, op0=ALU.mult)
        k.memset("dve", ssum, 0.0)
        k.act(Ps, pst[0:64, 0:256], AF.Exp, bias=nmx, scale=1.0, accum=ssum)
        k.recip(rinv, ssum)
        k.ts("dve", Ps, Ps, rinv, None, op0=ALU.mult)
        psp = pb[3]
        for mt in range(2):
            k.tr(psp[:, mt * 64:(mt + 1) * 64], Ps[:, mt * 128:(mt + 1) * 128], identf[0:64, 0:64])
        k.copy("dve", PTs.re("p a b -> p (a b)"), psp[:, 0:128])
        pso = pb[4]
        for b in range(B):
            vt32 = KV[(b + 1) % 5]
            k.dma("sp", vt32, cvv[b])
            vt = vbf[b % 2]
            k.act(vt[:, 0, :], vt32[:, 0, :], AF.Copy)
            k.copy("dve", vt[:, 1, :], vt32[:, 1, :])
            for c in range(8):
                h = c // 2
                for mt in range(2):
                    k.mm(pso[:, c * B + b:c * B + b + 1], vt[:, mt, c * 128:(c + 1) * 128], PTs[:, mt, b * 4 + h:b * 4 + h + 1], start=(mt == 0), stop=(mt == 1))
        k.copy("dve", oTs, pso[:, 0:8 * B].re("p (c b) -> p c b", c=8))
        tap("s_oT", oTs, [128, 8, B])
        for hf in range(2):
            ps = nextbig()
            for c in range(8):
                k.mm(ps[:B, :], oTs[:, c, :], wo[:, c, hf * 512:(hf + 1) * 512], start=(c == 0), stop=(c == 7))
            xs_ = xsr[:, hf * 512:(hf + 1) * 512]
            k.tt("dve", xs_, ps[:B, :], xs_, ALU.add)
        tap("s_x2", xsr, [B, 1024])


    def sample_ffn_prep():
        hT3 = A("hTs3", [128, 8, B], BF16)
        hidS = A("hidS", [128, 4, B], BF16)
        rls = A("rls", [128, B], F32)
        rms_T([xsr], B, G_FFN, hT3, 0)
        return hT3, hidS, rls

    def sample_ffn_block(st, wu_, wd_):
        hT3, hidS, rls = st
        for hc in range(4):
            ps = pb[hc % 4]
            for c in range(8):
                k.mm(ps[:, 0:B], wu_[:, c, hc * 128:(hc + 1) * 128], hT3[:, c, :], start=(c == 0), stop=(c == 7))
            k.act(rls, ps[:, 0:B], AF.Relu)
            k.act(hidS[:, hc, :], rls, AF.Square)
        for hf in range(2):
            ps = pb[4 + hf]
            for hc in range(4):
                k.mm(ps[:B, :], hidS[:, hc, :], wd_[:, hc, hf * 512:(hf + 1) * 512], start=(hc == 0), stop=(hc == 3))
            xs_ = xsr[:, hf * 512:(hf + 1) * 512]
            k.tt("dve", xs_, ps[:B, :], xs_, ALU.add)

    def sample_final(fgb_):
        yos = A("yos", [B, 1024], F32)
        k.memset("dve", ss, 0.0)
        k.act(junk[:B], xsr, AF.Square, accum=ss[:B, 0:1])
        k.act(rs[:B, 0:1], ss[:B, 0:1], AF.Sqrt, bias=EPS, scale=1.0 / 1024)
        k.recip(rs[:B, 0:1], rs[:B, 0:1])
        k.stt("dve", yos, xsr, rs[:B, 0:1], fgb_[:B], op0=ALU.mult, op1=ALU.mult)
        k.dma("sp", O["y_s"], yos)

    m0 = k.aoff
    WT = A("WT", [128, 4, 128], BF16)
    m0b = k.aoff
    win = A("win", [128, 8, 2816], BF16)
    wout = A("wout", [128, 8, 1024], BF16)
    for c in range(8):
        k.dma("pool", win[:, c, :], D["w_in"].re("(c p) n -> c p n", p=128)[c])
    for c in range(8):
        k.dma("pool", wout[:, c, :], D["w_out"].re("(c p) n -> c p n", p=128)[c])
    m_ws = k.aoff
    wsf = A("wsf", [128, 4, 128], F32)
    k.dma("sp", wsf, D["gm_ws"].re("h t s -> t h s"))
    wsb = A("wsb", [128, 4, 128], BF16)
    for h in range(4):
        k.aselect(wsf[:, h, :], wsf[:, h, :], [[-1, 128]], ALU.is_ge, 0.0, base=0, cm=1)
    k.copy("pool", wsb, wsf)
    ptv = nextbig().bitcast(BF16).re("p (c t) -> p c t", c=8)
    for h in range(4):
        k.tr(ptv[:, h, :], wsb[:, h, :], ident)
    k.copy("dve", WT, ptv[:, 0:4, :])
    k.barrier()
    k.aoff = m_ws

    m_w1 = k.aoff
    hT = A("hT", [128, 8, TT], BF16)
    uT = A("uT", [128, 4, TT], F32)
    vg = A("vg", [128, 512], F32)
    vnf = A("vnf", [128, 512], F32)
    vnb = A("vnb", [128, NCH, 512], BF16)
    mixTs_ = [A("mixT%d" % i, [128, 8, TT], BF16) for i in range(2)]
    tmpA = A("tmpA", [128, TT], F32)
    prevcol = A("prevcol", [128, 14], F32)
    k.memset("pool", prevcol, 0.0)
    twal = A("twal", [128, TT], BF16)
    sgl = A("sgl", [128, TT], BF16)
    PBx = A("PBx", [128, 2, 128], F32)
    tmpx = [A("tmpx%d" % i, [128, 4, 129], F32) for i in range(2)]
    dd = A("dd", [128, 4, 128], F32)
    I4 = A("I4", [128, 4, 128], BF16)
    k.copy("pool", I4, ident.un(1).bc([128, 4, 128]))
    segm4 = A("segm4", [128, 4, 128], F32)
    k.memset("pool", segm4, 1.0)
    k.memset("pool", segm4[:, :, 0:1], 0.0)
    AR = A("AR", [128, 4, NCH, 2, 128], BF16)
    BKT = A("BKT", [128, 4, 2, TT], BF16)
    VT = A("VT", [128, 4, TT], BF16)
    rkr = A("rkr", [128, 4, TT], BF16)
    PC = A("PC", [128, 4, NCH], F32)
    Btok = A("Btok", [128, 512], BF16)
    Ktok = A("Ktok", [128, 512], BF16)
    Vtok = A("Vtok", [128, 512], BF16)
    LrbT = A("LrbT", [128, 8, 128], BF16)
    LT2 = A("LT2", [128, 8, 256], BF16)
    Mb = [A("Mb%d" % g, [128, 4, 128], BF16) for g in range(2)]
    MTP = [A("MTP%d" % g, [128, 4, 2, 128], BF16) for g in range(2)]
    Pf = [A("Pf%d" % g, [128, 4, 128], F32) for g in range(2)]
    Xb = A("Xb", [128, 8, 64], BF16)
    Ub = A("Ub", [128, 8, 64], BF16)
    Hs = A("Hs", [128, 4, 64], F32)
    Hb = [A("Hb%d" % i, [128, 4, 64], BF16) for i in range(2)]
    tmpH = A("tmpH", [128, 4, 64], F32)
    PBr = k.alias(Pf[0], [128, 4, 128])
    PBk = k.alias(Pf[1], [128, 4, 128])
    PBv = k.alias(MTP[0], [128, 4, 128])
    sg = k.alias(MTP[1], [128, 4, 128])
    asig = k.alias(LT2, [128, 4, 128], off=0)
    cum = k.alias(LT2, [128, 4, 128], off=512)
    Ex = k.alias(LrbT, [128, 4, 128])
    kk = k.alias(vg, [128, 4, 128])
    kmod = k.alias(vnf, [128, 4, 128])
    t1 = k.alias(sqs, [128, 4, 128])
    t2 = k.alias(uT, [128, 4, 128])
    Ysb = vg
    yn = vnf
    bon = A("bon", [128, 8], F32)
    ob = A("ob", [128, 512], BF16)
    k.memset("pool", Hs, 0.0)
    k.memset("pool", Hb[0], 0.0)
    hb_i = [0]
    m2f = mask2.re("p a t -> p (a t)")

    pg_i = [0]

    def proj_group(c0_, n, out3):
        ps = nextbig()
        for i in range(n):
            m = c0_ + i
            for c in range(8):
                k.mm(ps[:, i * 128:(i + 1) * 128], win[:, c, 1024 + m * 128:1024 + (m + 1) * 128], hT[:, c, :], start=(c == 0), stop=(c == 7))
        tx = tmpx[pg_i[0] % 2]
        pg_i[0] += 1
        k.copy("dve", tx[:, 0:n, 0:1], prevcol[:, c0_:c0_ + n].un(2))
        k.act(tx[:, 0:n, 1:129], ps[:, 0:n * 128].re("p (c t) -> p c t", c=n), AF.Copy)
        k.copy("dve", prevcol[:, c0_:c0_ + n].un(2), tx[:, 0:n, 128:129])
        k.tt("dve", dd[:, 0:n, :], tx[:, 0:n, 0:128], tx[:, 0:n, 1:129], ALU.subtract)
        k.tt("dve", dd[:, 0:n, :], dd[:, 0:n, :], mu[:, c0_:c0_ + n].un(2).bc([128, n, 128]), ALU.mult)
        k.tt("dve", out3, dd[:, 0:n, :], tx[:, 0:n, 1:129], ALU.add)

    NFILL_A = int(os.environ.get('KFILLA', '0'))
    NFILL_B = int(os.environ.get('KFILLB', '0'))
    I4f = I4.re("p a t -> p (a t)")

    def pe_fill(n, bank):
        for _ in range(n):
            k.mm(bank, ident, I4f, start=True, stop=True)

    pending_wout = []

    def flush_wout():
        while pending_wout:
            T_, mx_ = pending_wout.pop(0)
            for hf in range(2):
                ps = nextbig()
                for c in range(8):
                    k.mm(ps, mx_[:, c, :], wout[:, c, hf * 512:(hf + 1) * 512], start=(c == 0), stop=(c == 7))
                xs_ = xr[T_][:, hf * 512:(hf + 1) * 512]
                k.tt("dve", xs_, ps, xs_, ALU.add)

    v2 = lambda tv: tv.re("p (j t) -> p j t", j=NCH)
    ntiles = NT if stages >= 3 else (2 if stages >= 2 else 1)
    junk2 = k.alias(tmpx[1], [128, 1024], BF16)
    rsN = A("rsN", [128, 8], F32)
    ssN = A("ssN", [128, 8], F32)
    pre_rms = [False]
    for T in range(ntiles):
        mixT = mixTs_[T % 2]
        for j_ in ((2, 3) if T == 0 else (T + 3,)):
            if j_ < 16:
                k.dma("sp", xr[j_], xpv[j_])
        if not pre_rms[0]:
            rms_T([xr[T]], 128, G_MIX, hT, 0)
        pre_rms[0] = False
        if T == 0:
            tap("hT", hT, [128, 8, TT])
        def mixA_u(ms):
            for m in ms:
                ps = nextbig()
                for c in range(8):
                    k.mm(ps[:, 0:TT], win[:, c, m * 128:(m + 1) * 128], hT[:, c, :], start=(c == 0), stop=(c == 7))
                k.act(uT[:, m, :], ps[:, 0:TT], AF.Gelu)

        def mixA_v():
            ps = nextbig()
            for c in range(8):
                k.mm(ps, hT[:, c, 0:128], win[:, c, 512:1024], start=(c == 0), stop=(c == 7))
            k.act(vg, ps, AF.Gelu)

        def mixA_ln():
            ln_heads(128, vg, 4, 128, 1e-5, bct[:, 0, :], bct[:, 1, :], vnf, sqs)
            k.act(vnb[:, 0, :], vnf, AF.Copy)
            if T == NT - 1:
                k.dma("sp", O["cvp_o"], vnf)
            if T == 0:
                tap("vnf", vnf, [128, 512])

        def mixA_gate(hs):
            for h in hs:
                ps = nextbig()
                k.mm(ps[:, 0:128], vnb[:, 0, h * 128:(h + 1) * 128], WT[:, h, :])
                k.tt("dve", tmpA, ps[:, 0:TT], bct[:, 4, h * 128:(h + 1) * 128], ALU.add)
                k.tt("dve", mixT[:, h, :], tmpA, uT[:, h, :], ALU.mult)

        mixA_sched = {0: lambda: mixA_u([0, 1]), 1: lambda: mixA_u([2, 3]), 2: mixA_v, 3: mixA_ln,
                      4: lambda: mixA_gate([0, 1]), 5: lambda: mixA_gate([2, 3])}
        if stages <= 1:
            for i_ in range(6):
                mixA_sched[i_]()
            if T == 0:
                tap("mixTa", mixT, [128, 8, TT])
        if stages <= 1:
            break
        bc4 = lambda col: col.un(2).bc([128, 4, 128])
        f2d = lambda tv: tv.re("p a t -> p (a t)")
        re4 = lambda ps_: ps_.re("p (a t) -> p a t", a=4)
        proj_group(12, 2, PBx)
        k.act(twal[0:64, :], PBx[0:64, 0, :], AF.Tanh)
        k.act(twal[64:128, :], PBx[64:128, 0, :], AF.Copy)
        k.act(sgl, PBx[:, 1, :], AF.Sigmoid)
        proj_group(0, 4, PBr)
        proj_group(4, 4, PBk)
        proj_group(8, 4, PBv)
        psw = nextbig()
        for p in range(4):
            k.mm(psw[:, p * 128:(p + 1) * 128], w2a2[0:64, p * 128:(p + 1) * 128], twal[0:64, :])
        psa_ = nextbig()
        for p in range(4):
            k.mm(psa_[:, p * 128:(p + 1) * 128], w2a2[64:128, p * 128:(p + 1) * 128], twal[64:128, :])
        k.tt("dve", sg, re4(psw), bc4(pw0), ALU.add)
        k.act(sg, sg, AF.Sigmoid)
        k.tt("dve", asig, re4(psa_), bc4(pa0), ALU.add)
        k.act(asig, asig, AF.Sigmoid)
        k.scan(f2d(cum), f2d(segm4), f2d(sg))
        k.act(Ex, cum, AF.Exp, scale=-math.exp(-0.5))
        k.tt("pool", AR[:, :, 0, 1, :], PBr, Ex, ALU.mult)
        k.copy("dve", PC[:, :, 0:1], Ex[:, :, 127:128])
        k.tt("dve", t1, cum, sg, ALU.subtract)
        k.tt("dve", kk, PBk, bc4(pkk), ALU.mult)
        k.act(t2, kk, AF.Square)
        pss = nextbig()
        for p in range(4):
            k.mm(pss[:, p * 128:(p + 1) * 128], blk2, t2[:, p, :])
        pe_fill(NFILL_A, pb[0])
        k.act(Ex, t1, AF.Exp, scale=-math.exp(-0.5))
        k.act(t2, re4(pss), AF.Sqrt)
        k.ts("dve", t2, t2, 1e-12, None, op0=ALU.max)
        k.recip(t2, t2)
        k.tt("dve", kk, kk, t2, ALU.mult)
        k.stt("dve", AR[:, :, 0, 0, :], kk, -1.0, Ex, op0=ALU.mult, op1=ALU.mult)
        k.stt("dve", t1, asig, -1.0, bc4(pka), op0=ALU.add, op1=ALU.mult)
        k.act(Ex, cum, AF.Exp, scale=math.exp(-0.5))
        k.ts("dve", t1, t1, 1.0, None, op0=ALU.add)
        k.tt("dve", kmod, PBk, t1, ALU.mult)
        k.tt("dve", t1, kk, asig, ALU.mult)
        k.tt("dve", BKT[:, :, 0, :], t1, Ex, ALU.mult)
        k.tt("pool", BKT[:, :, 1, :], kmod, Ex, ALU.mult)
        k.act(VT, PBv, AF.Copy)
        k.tt("pool", t2, PBr, kmod, ALU.mult)
        k.tt("pool", rkr, t2, bc4(prk), ALU.mult)
        if T == 0:
            tap("kk0", kk, [128, 4, 128])
            tap("kmod", kmod, [128, 4, 128])
            tap("asig", asig, [128, 4, 128])
            tap("rS", PBr, [128, 4, 128])
        if CUT == 1:
            return k, tapd
        for j in range(NCH):
            js = slice(j * 128, (j + 1) * 128)
            def tok_tr(src, dst):
                ptv_ = nextbig().bitcast(BF16).re("p (c t) -> p c t", c=8)
                for p_ in range(4):
                    k.tr(ptv_[:, p_, :], src[:, p_, js], ident)
                k.copy("dve", dst.re("p (c t) -> p c t", c=4), ptv_[:, 0:4, :])
            def pre_a():
                if T + 1 < ntiles and stages >= 3:
                    rms_A([xr[T + 1]], 128, [xsb[0]], rsN, junk_=junk2, ss_=ssN)

            def pre_b():
                flush_wout()
                if T + 1 < ntiles and stages >= 3:
                    rms_B(1, 128, G_MIX, hT, 0, [xsb[0]])
                    pre_rms[0] = True
            fill = {0: lambda: tok_tr(BKT[:, :, 0, :], Btok), 1: lambda: tok_tr(BKT[:, :, 1, :], Ktok),
                    2: lambda: tok_tr(VT, Vtok), 4: pre_a, 6: pre_b}
            if CUT == 2:
                return k, tapd
            for g in range(2):
                psa = [pb[0 + 3 * g], pb[1 + 3 * g]]
                hd = []
                for i in range(4):
                    h = 4 * g + i
                    hd.append((h // 2, slice((h % 2) * 64, (h % 2 + 1) * 64), i % 2, i // 2))
                for (p, hp, e, sl) in hd:
                    k.mm(psa[e][:, sl * 256:(sl + 1) * 256], BKT[hp, p, 0, js], AR[hp, p, j].re("p a t -> p (a t)"))
                for e in range(2):
                    pv = psa[e].re("p (i a t) -> p i a t", i=2, a=2)
                    k.tt("dve", MTP[g][:, e::2, 0, :], pv[:, :, 0, :], mask2[:, 0, :].un(1).bc([128, 2, 128]), ALU.mult)
                    k.tt("dve", LrbT[:, 4 * g + e:4 * g + 4:2, :], pv[:, :, 1, :], mask2[:, 1, :].un(1).bc([128, 2, 128]), ALU.mult)
                for (p, hp, e, sl) in hd:
                    k.mm(psa[e][:, sl * 256:(sl + 1) * 256], BKT[hp, p, 1, js], AR[hp, p, j].re("p a t -> p (a t)"))
                for e in range(2):
                    k.tt("dve", LT2[:, 4 * g + e:4 * g + 4:2, :], psa[e].re("p (i x) -> p i x", i=2), m2f.un(1).bc([128, 2, 256]), ALU.mult)
                for (p, hp, e, sl) in hd:
                    k.mm(psa[e][:, sl * 128:(sl + 1) * 128], AR[hp, p, j, 0, :], BKT[hp, p, 0, js])
                for e in range(2):
                    k.tt("dve", Mb[g][:, e::2, :], psa[e][:, 0:256].re("p (a t) -> p a t", a=2), maskN.un(1).bc([128, 2, 128]), ALU.mult)
                k.copy("dve", MTP[g][:, :, 1, :], ident.un(1).bc([128, 4, 128]))
            if CUT == 3:
                return k, tapd
            for g in range(2):
                k.mm(pb[2 + 3 * g], ident, I4.re("p a t -> p (a t)"), start=True, stop=True)
            for lev in range(7):
                for g in range(2):
                    psA, psB, psP = pb[0 + 3 * g], pb[1 + 3 * g], pb[2 + 3 * g]
                    mt = MTP[g]
                    if lev < 6:
                        for i in range(4):
                            k.mm(psA[:, i * 128:(i + 1) * 128], mt[:, i, 0, :], Mb[g][:, i, :])
                        for i in range(4):
                            k.mm(psB[:, i * 128:(i + 1) * 128], Mb[g][:, i, :], mt[:, i, 0, :])
                    for i in range(4):
                        k.mm(psP[:, i * 128:(i + 1) * 128], Mb[g][:, i, :], mt[:, i, 1, :], start=False, stop=True)
                    if lev < 6:
                        k.act(Mb[g], psA.re("p (a t) -> p a t", a=4), AF.Copy)
                        k.act(mt[:, :, 0, :], psB.re("p (a t) -> p a t", a=4), AF.Copy)
                    k.copy("dve", mt[:, :, 1, :], psP.re("p (a t) -> p a t", a=4))
                if lev in mixA_sched:
                    mixA_sched[lev]()
                if lev in fill:
                    fill[lev]()
            if CUT == 4:
                return k, tapd
            hbo = Hb[hb_i[0] % 2]
            hbn = Hb[(hb_i[0] + 1) % 2]
            hb_i[0] += 1
            psXe, psU, psH, psYe = [pb[0], pb[1]], pb[2], pb[3], [pb[4], pb[5]]
            for h in range(8):
                p, e = h // 2, h % 2
                hp = slice(e * 64, (e + 1) * 64)
                k.mm(psXe[e][:, p * 64:(p + 1) * 64], LT2[:, h, 0:128], Vtok[:, h * 64:(h + 1) * 64], start=True, stop=False)
                k.mm(psXe[e][:, p * 64:(p + 1) * 64], AR[hp, p, j, 0, :], hbo[hp, p, :], start=False, stop=True)
            for e in range(2):
                k.act(Xb[:, e::2, :], psXe[e][:, 0:256].re("p (q v) -> p q v", q=4), AF.Copy)
            for h in range(8):
                k.mm(psU[:, h * 64:(h + 1) * 64], MTP[h // 4][:, h % 4, 1, :], Xb[:, h, :])
            k.act(Ub, psU.re("p (h v) -> p h v", h=8), AF.Copy)
            for h in range(8):
                p, e = h // 2, h % 2
                hp = slice(e * 64, (e + 1) * 64)
                k.mm(psYe[e][:, p * 64:(p + 1) * 64], LrbT[:, h, :], Ub[:, h, :], start=True, stop=False)
                k.mm(psYe[e][:, p * 64:(p + 1) * 64], LT2[:, h, 128:256], Vtok[:, h * 64:(h + 1) * 64], start=False, stop=False)
                k.mm(psYe[e][:, p * 64:(p + 1) * 64], AR[hp, p, j, 1, :], hbo[hp, p, :], start=False, stop=True)
            for h in range(8):
                p = h // 2
                k.mm(psH[:, h * 64:(h + 1) * 64], Btok[:, p * 128:(p + 1) * 128], Ub[:, h, :], start=True, stop=False)
                k.mm(psH[:, h * 64:(h + 1) * 64], Ktok[:, p * 128:(p + 1) * 128], Vtok[:, h * 64:(h + 1) * 64], start=False, stop=True)
            psH4 = psH.re("p (q e v) -> p q e v", q=4, e=2)
            for e in range(2):
                hp = slice(e * 64, (e + 1) * 64)
                k.tt("dve", tmpH[hp], psH4[hp, :, e, :], Hs[hp], ALU.add)
                k.tt("dve", Hs[hp], tmpH[hp], PC[hp, :, j:j + 1].bc([64, 4, 64]), ALU.mult)
            k.copy("dve", hbn, Hs)
            if CUT == 5:
                return k, tapd
            for e in range(2):
                k.act(Ysb.re("p (q e v) -> p q e v", q=4, e=2)[:, :, e, :], psYe[e][:, 0:256].re("p (q v) -> p q v", q=4), AF.Copy)
            if T == 0 and j == 0:
                tap("Y", Ysb, [128, 512])
            ln_heads(128, Ysb, 8, 64, 64e-5, bct[:, 2, :], bct[:, 3, :], yn, sqs)
            psb = nextbig()
            for p in range(4):
                k.mm(psb[:, 2 * p:2 * p + 2], rkr[:, p, js], hsel)
            k.copy("dve", bon, psb[:, 0:8])
            k.tt("dve", sqs.re("p (h v) -> p h v", h=8), Vtok.re("p (h v) -> p h v", h=8), bon.un(2).bc([128, 8, 64]), ALU.mult)
            k.tt("dve", yn, yn, sqs, ALU.add)
            psg = nextbig()
            k.mm(psg, sgl[:, js], g2)
            pe_fill(NFILL_B, pb[0])
            k.tt("dve", ob, yn, psg, ALU.mult)
            ptv = nextbig().bitcast(BF16).re("p (c t) -> p c t", c=8)
            for p in range(4):
                k.tr(ptv[:, p, :], ob[:, p * 128:(p + 1) * 128], ident)
            k.copy("dve", mixT[:, 4:8, js], ptv[:, 0:4, :])
        if T == 0:
            tap("mixT", mixT, [128, 8, TT])
        if CUT == 6:
            return k, tapd
        pending_wout.append((T, mixT))
        if stages <= 2 or T == ntiles - 1:
            flush_wout()
        if T == 0:
            tap("x1", xr[0], [128, 1024])
        if T == 1:
            tap("x1b", xr[1], [128, 1024])
    if CUT == 7:
        return k, tapd
    if stages >= 2:
        k.dma("sp", O["shp_o"].re("(c p) -> p c", p=128), prevcol, noncontig=True)
        wko = A("wko", [64, 8, 64], F32)
        for e in range(2):
            hp = slice(e * 64, (e + 1) * 64)
            psT = pb[e].re("p (h k) -> p h k", h=8)
            for p in range(4):
                k.tr(psT[0:64, p, :], Hs[hp, p, :], identf[hp, hp])
            k.copy("dve", wko[:, e::2, :], psT[0:64, 0:4, :])
        k.dma("sp", O["wkvp_o"].re("h v k -> v h k"), wko)
    k.barrier()
    if stages >= 9:
        k.aoff = m_w1
        sample_S1()
        k.barrier()
    k.aoff = m0b
    if stages <= 3:
        return k, tapd

    wq = A("wq", [128, 8, 1024], BF16)
    wo = A("wo", [128, 8, 1024], BF16)
    if stages < 9:
        for c in range(8):
            k.dma("pool", wq[:, c, :], D["w_xq"].re("(c p) n -> c p n", p=128)[c])
        for c in range(8):
            k.dma("pool", wo[:, c, :], D["w_xo"].re("(c p) n -> c p n", p=128)[c])
    mkT = A("mkT", [128, 8, 256], BF16)
    mvb = A("mvb", [128, 2, 1024], BF16)
    m1b = k.aoff
    wk = A("wk", [128, 8, 1024], BF16)
    wv = A("wv", [128, 8, 1024], BF16)
    for c in range(8):
        k.dma("pool", wk[:, c, :], D["w_xk"].re("(c p) n -> c p n", p=128)[c])
        k.dma("pool", wv[:, c, :], D["w_xv"].re("(c p) n -> c p n", p=128)[c])
    memx = [A("memx%d" % j, [128, 1024], F32) for j in range(2)]
    for j in range(2):
        k.dma("sp", memx[j], D["mem"].re("(j p) d -> j p d", p=128)[j])
    mnT = A("mnT", [128, 8, 256], BF16)
    rms_T(memx, 128, G_MEM, mnT, 0)
    mko = [A("mko%d" % j, [128, 1024], F32) for j in range(2)]
    mkb = A("mkb", [128, 1024], BF16)
    for which, (w, od) in enumerate(((wk, O["mk_o"]), (wv, O["mv_o"]))):
        for j in range(2):
            for hf in range(2):
                ps = nextbig()
                for c in range(8):
                    k.mm(ps, mnT[:, c, j * 128:(j + 1) * 128], w[:, c, hf * 512:(hf + 1) * 512], start=(c == 0), stop=(c == 7))
                k.act(mko[j][:, hf * 512:(hf + 1) * 512], ps, AF.Copy)
            k.dma("sp", od.re("(j p) d -> j p d", p=128)[j], mko[j])
            if which == 0:
                k.copy("pool", mkb, mko[j])
                ptv = nextbig().bitcast(BF16).re("p (c t) -> p c t", c=8)
                for c in range(8):
                    k.tr(ptv[:, c, :], mkb[:, c * 128:(c + 1) * 128], ident)
                k.copy("dve", mkT[:, :, j * 128:(j + 1) * 128], ptv)
            else:
                k.copy("pool", mvb[:, j, :], mko[j])
    k.barrier()
    k.aoff = m1b

    hT2s = [A("hT2_%d" % i, [128, 8, 512], BF16) for i in range(2)]
    qTs = [A("qT_%d" % i, [128, 8, 512], BF16) for i in range(2)]
    ef = [A("ef%d" % i, [128, 4, 256], BF16) for i in range(2)]
    Pn = [A("Pn%d" % i, [128, 4, 256], BF16) for i in range(2)]
    PT = A("PT", [128, 8, 512], BF16)
    oT = A("oT", [128, 8, 512], BF16)
    smx = [A("smx%d" % i, [128, 16], F32) for i in range(2)]

    def softmax_rows(pt, banks, nh_per_bank, M, e_out, p_out, sm):
        H = nh_per_bank * len(banks)
        mx, nmx, ssum, rinv = (sm[:pt, i * 4:i * 4 + H] for i in range(4))
        for b, psb_ in enumerate(banks):
            k.red("dve", mx[:, b * nh_per_bank:(b + 1) * nh_per_bank], psb_[:pt, 0:nh_per_bank * M].re("p (h m) -> p h m", h=nh_per_bank), op=ALU.max)
        k.ts("dve", nmx, mx, -1.0, None, op0=ALU.mult)
        k.memset("dve", ssum, 0.0)
        for h in range(H):
            b, hh = h // nh_per_bank, h % nh_per_bank
            k.act(e_out[:pt, h, :], banks[b][:pt, hh * M:(hh + 1) * M], AF.Exp, bias=nmx[:, h:h + 1], scale=1.0, accum=ssum[:, h:h + 1])
        k.recip(rinv, ssum)
        k.tt("dve", p_out[:pt], e_out[:pt], rinv.un(2).bc([pt, H, M]), ALU.mult)

    ngrp = 4 if (stages >= 5 and NT == 16) else 1
    ev = [0]
    nsub = 4 if NT >= 4 else 1
    NTOK = nsub * 128

    xb4 = [A("xb4_%d" % i, [128, 1024], BF16) for i in range(4)]
    rs4 = A("rs4", [128, 8], F32)

    def qprojA(G):
        rms_A([xr[4 * G + i] for i in range(nsub)], 128, xb4, rs4)

    def qproj(G):
        hT = hT2s[G % 2]
        qT = qTs[G % 2]
        rms_B(nsub, 128, G_X, hT, 0, xb4)
        for m in range(8):
            ps = pb[m % 4]
            for c in range(8):
                k.mm(ps[:, 0:NTOK], wq[:, c, m * 128:(m + 1) * 128], hT[:, c, 0:NTOK], start=(c == 0), stop=(c == 7))
            if m % 2 == 0:
                k.ts("dve", qT[:, m, 0:NTOK], ps[:, 0:NTOK], 0.0625, None, op0=ALU.mult)
            else:
                k.act(qT[:, m, 0:NTOK], ps[:, 0:NTOK], AF.Copy, scale=0.0625)

    oTs = [oT, A("oT_b", [128, 8, 512], BF16)]

    def scores(G, subs):
        qT = qTs[G % 2]
        for s_ in subs:
            pss = [pb[4 + 2 * (s_ % 2)], pb[5 + 2 * (s_ % 2)]]
            for h in range(4):
                for dc in range(2):
                    k.mm(pss[h // 2][:, (h % 2) * 256:(h % 2 + 1) * 256], qT[:, 2 * h + dc, s_ * 128:(s_ + 1) * 128], mkT[:, 2 * h + dc, :], start=(dc == 0), stop=(dc == 1))

    def smax(subs):
        for s_ in subs:
            pss = [pb[4 + 2 * (s_ % 2)], pb[5 + 2 * (s_ % 2)]]
            softmax_rows(128, pss, 2, 256, ef[s_ % 2], Pn[s_ % 2], smx[s_ % 2])

    def ptrans(subs):
        for s_ in subs:
            ptv = pb[s_ % 2].bitcast(BF16).re("p (c t) -> p c t", c=8)
            for h in range(4):
                for mt in range(2):
                    k.tr(ptv[:, h * 2 + mt, :], Pn[s_ % 2][:, h, mt * 128:(mt + 1) * 128], ident)
            k.copy("dve" if s_ % 2 == 0 else "act", PT[:, :, s_ * 128:(s_ + 1) * 128], ptv)

    def w_o(G):
        oTg = oTs[G % 2]
        for s_ in range(nsub):
            for hf in range(2):
                ps = pb[(2 * s_ + hf) % 4]
                for c in range(8):
                    k.mm(ps, oTg[:, c, s_ * 128:(s_ + 1) * 128], wo[:, c, hf * 512:(hf + 1) * 512], start=(c == 0), stop=(c == 7))
                xs_ = xr[4 * G + s_][:, hf * 512:(hf + 1) * 512]
                k.tt("dve", xs_, ps, xs_, ALU.add)

    qprojA(0)
    qproj(0)
    pend = None
    for G in range(ngrp):
        r1 = [s_ for s_ in (0, 1) if s_ < nsub]
        r2 = [s_ for s_ in (2, 3) if s_ < nsub]
        scores(G, r1)
        smax(r1)
        if pend is not None:
            w_o(pend)
        ptrans(r1)
        if G + 1 < ngrp:
            qprojA(G + 1)
        if r2:
            scores(G, r2)
            smax(r2)
        if G + 1 < ngrp:
            qproj(G + 1)
        if r2:
            ptrans(r2)
        oTg = oTs[G % 2]
        for c in range(8):
            h = c // 2
            ps = pb[c % 4]
            for mt in range(2):
                k.mm(ps[:, 0:NTOK], mvb[:, mt, c * 128:(c + 1) * 128], PT[:, h * 2 + mt, 0:NTOK], start=(mt == 0), stop=(mt == 1))
            if c % 2 == 0:
                k.act(oTg[:, c, 0:NTOK], ps[:, 0:NTOK], AF.Copy)
            else:
                k.copy("dve", oTg[:, c, 0:NTOK], ps[:, 0:NTOK])
        pend = G
    w_o(pend)
    tap("x2", xr[0], [128, 1024])
    k.barrier()
    if stages >= 9:
        k.aoff = m1b
        sample_S2()
        k.barrier()
    k.aoff = m0b
    if stages <= 5:
        return k, tapd

    NB = 8
    xnT = A("xnT", [128, 8, 2048], BF16)
    wu = [A("wu%d" % i, [128, 8, 512], BF16) for i in range(2)]
    wd = [A("wd%d" % i, [128, 4, 1024], BF16) for i in range(2)]
    hidT = [A("hidT%d" % i, [128, 4, 512], BF16) for i in range(2)]
    rl = [A("rl%d" % i, [128, 512], F32) for i in range(2)]

    hidT = [A("hidT%d" % i, [128, 4, 512], BF16) for i in range(2)]
    rl = [A("rl%d" % i, [128, 512], F32) for i in range(2)]

    def ffn_load(b):
        k.dma("pool", wu[b % 2], D["w_up"].re("(c p) n -> p c n", p=128)[:, :, b * 512:(b + 1) * 512])
        k.dma("pool", wd[b % 2], D["w_down"].re("(q p) n -> p q n", p=128)[:, b * 4:(b + 1) * 4, :])

    ffn_load(0)
    ffn_load(1)
    fgb = A("fgb", [128, 1024], F32)
    k.dma("sp", fgb, TV(D["final_g"].ap.partition_broadcast(128), None))
    yo = [A("yo%d" % i, [128, 1024], F32) for i in range(2)]
    ypv = O["y_p"].re("(j p) d -> j p d", p=128)

    def final_tile(T_):
        k.memset("pool", ss, 0.0)
        k.act(junk, xr[T_], AF.Square, accum=ss[:, 0:1])
        k.act(rs[:, 0:1], ss[:, 0:1], AF.Sqrt, bias=EPS, scale=1.0 / 1024)
        k.recip(rs[:, 0:1], rs[:, 0:1])
        k.stt("dve", yo[T_ % 2], xr[T_], rs[:, 0:1], fgb, op0=ALU.mult, op1=ALU.mult)
        k.dma("sp", ypv[T_], yo[T_ % 2])

    for G in range(NT // 4):
        rms_T([xr[4 * G + i] for i in range(4)], 128, G_FFN, xnT, G * 512)
    nblk = (NB if stages >= 7 else 1) if NT == 16 else 0
    sst = sample_ffn_prep() if stages >= 9 else None
    cu = [0]
    cd = [0]

    def ffn_up(b, tg):
        hid = hidT[(b * 4 + tg) % 2]
        for hc in range(4):
            ps = pb[cu[0] % 4]
            r_ = rl[cu[0] % 2]
            cu[0] += 1
            for c in range(8):
                k.mm(ps, wu[b % 2][:, c, hc * 128:(hc + 1) * 128], xnT[:, c, tg * 512:(tg + 1) * 512], start=(c == 0), stop=(c == 7))
            k.act(r_, ps, AF.Relu)
            k.act(hid[:, hc, :], r_, AF.Square)

    def ffn_down(b, tg):
        hid = hidT[(b * 4 + tg) % 2]
        for sub in range(4):
            T_ = tg * 4 + sub
            for hf in range(2):
                ps = pb[4 + cd[0] % 4]
                cd[0] += 1
                for hc in range(4):
                    k.mm(ps, hid[:, hc, sub * 128:(sub + 1) * 128], wd[b % 2][:, hc, hf * 512:(hf + 1) * 512], start=(hc == 0), stop=(hc == 3))
                xs_ = xr[T_][:, hf * 512:(hf + 1) * 512]
                k.tt("dve", xs_, ps, xs_, ALU.add)
            if b == nblk - 1 and nblk == NB:
                final_tile(T_)

    steps = [(b, tg) for b in range(nblk) for tg in range(4)]
    for i, (b, tg) in enumerate(steps):
        ffn_up(b, tg)
        if i > 0:
            ffn_down(*steps[i - 1])
            pb_, ptg = steps[i - 1]
            if ptg == 3:
                if sst is not None:
                    sample_ffn_block(sst, wu[pb_ % 2], wd[pb_ % 2])
                if pb_ + 2 < nblk:
                    ffn_load(pb_ + 2)
    if steps:
        ffn_down(*steps[-1])
        if sst is not None:
            sample_ffn_block(sst, wu[steps[-1][0] % 2], wd[steps[-1][0] % 2])
    tap("x3", xr[0], [128, 1024])

    if nblk != NB:
        for T in range(NT):
            final_tile(T)
    if stages >= 9:
        sample_final(fgb)
    k.barrier()
    k.aoff = m0b
    if stages <= 8:
        return k, tapd
    return k, tapd


def _in_maps(inputs, n=8):
    maps = []
    for c in range(n):
        m = {}
        m["xp"] = np.ascontiguousarray(inputs["x_prompt"][c])
        m["xs"] = np.ascontiguousarray(inputs["x_sample"][16 * c:16 * c + 16, 0])
        m["mem"] = np.ascontiguousarray(inputs["mem_prompt"][c])
        m["ck"] = np.ascontiguousarray(inputs["cache_mem_k"][0, 16 * c:16 * c + 16]).reshape(16, 256, 1024)
        m["cv"] = np.ascontiguousarray(inputs["cache_mem_v"][0, 16 * c:16 * c + 16]).reshape(16, 256, 1024)
        m["ssh"] = np.ascontiguousarray(inputs["state_shift"][0, 16 * c:16 * c + 16])
        m["swkv"] = np.ascontiguousarray(inputs["state_wkv"][0, 16 * c:16 * c + 16])
        for nme in W_NAMES:
            a = np.asarray(inputs[nme])
            if nme != "final_g":
                a = a[0]
            m[nme] = np.ascontiguousarray(a.reshape(W_SHAPES[nme]))
        maps.append(m)
    return maps


_NC_CACHE = {}


def kernel(**inputs):
    inputs = {k_: np.asarray(v_) for k_, v_ in inputs.items()}
    if "nc" not in _NC_CACHE:
        kb, _ = build()
        _NC_CACHE["nc"] = kb.finish()
    nc = _NC_CACHE["nc"]
    res = run_bass_kernel_spmd(nc, _in_maps(inputs), core_ids=list(range(8))).results
    cat = lambda n_: np.stack([np.asarray(r[n_]) for r in res])
    y_p = cat("y_p")
    y_s = cat("y_s").reshape(128, 1, 1024)
    mk = cat("mk_o").reshape(1, 8, 256, 4, 256)
    mv = cat("mv_o").reshape(1, 8, 256, 4, 256)
    shp = cat("shp_o").reshape(1, 8, 1792)
    wkvp = cat("wkvp_o").reshape(1, 8, 8, 64, 64)
    cvp = cat("cvp_o").reshape(1, 8, 128, 4, 128)
    shs = cat("shs_o").reshape(1, 128, 1792)
    wkvs = cat("wkvs_o").reshape(1, 128, 8, 64, 64)
    cvs = cat("cvs_o").reshape(1, 128, 1, 4, 128)
    return tuple(np.ascontiguousarray(a, dtype=np.float32) for a in (y_p, y_s, mk, mv, shp, wkvp, cvp, shs, wkvs, cvs))
```

```python
import numpy as np
import concourse.bass as bass
import concourse.mybir as mybir
from concourse.bass_utils import run_bass_kernel_spmd

F32 = mybir.dt.float32
BF16 = mybir.dt.bfloat16
AF = mybir.ActivationFunctionType
ALU = mybir.AluOpType
AX = mybir.AxisListType


class Buf:
    def __init__(self, name):
        self.name = name
        self.last_w = None
        self.readers = []
        self.sem = None
        self.cnt = 0


class TV:
    def __init__(self, ap, buf):
        self.ap = ap
        self.buf = buf

    def __getitem__(self, key):
        return TV(self.ap[key], self.buf)

    def re(self, s, **kw):
        return TV(self.ap.rearrange(s, **kw), self.buf)

    def bc(self, shape):
        return TV(self.ap.to_broadcast(shape), self.buf)

    def un(self, ax):
        return TV(self.ap.unsqueeze(ax), self.buf)

    def bitcast(self, dt):
        return TV(self.ap.bitcast(dt), self.buf)

    def wb(self, buf):
        return TV(self.ap, buf)


class Op:
    __slots__ = ("eng", "fn", "reads", "writes", "dma", "deps", "sig", "ticket", "sem", "val", "idx")


ENGS = ("pe", "act", "dve", "pool", "sp")


class KB:
    def __init__(self):
        self.nc = bass.Bass("TRN2", target_bir_lowering=False)
        self.ops = []
        self.ctx = []
        self.stores = []
        self.dram_bufs = {}

    def enter(self, cm):
        v = cm.__enter__()
        self.ctx.append(cm)
        return v

    def sb(self, name, shape, dt=F32):
        t = self.enter(self.nc.sbuf_tensor(name, list(shape), dt))
        return TV(t[:], Buf(name))

    def ps(self, name, shape, dt=F32):
        t = self.enter(self.nc.psum_tensor(name, list(shape), dt))
        return TV(t[:], Buf(name))

    def dram(self, name, shape, dt=F32, kind="ExternalInput"):
        t = self.nc.dram_tensor(name, list(shape), dt, kind=kind)
        return TV(t.ap(), None)

    def arena_init(self, nbytes):
        self.arena = self.enter(self.nc.sbuf_tensor("arena", [128, nbytes // 4], F32))
        self.arena_size = nbytes
        self.aoff = 0
        self.phase = None
        self.amax = 0
        self._bar_ap = self.alloc("bar", [128, 8]).ap
        self.phase = Buf("phase")

    def alloc(self, name, shape, dt=F32):
        esz = 2 if dt == BF16 else 4
        n = esz
        for s in shape[1:]:
            n *= s
        n = (n + 31) // 32 * 32
        assert self.aoff + n <= self.arena_size, "arena overflow at %s: %d + %d > %d" % (name, self.aoff, n, self.arena_size)
        ap = self.arena[:, self.aoff // 4:(self.aoff + n) // 4]
        raw = ap
        self.aoff += n
        self.amax = max(self.amax, self.aoff)
        if dt == BF16:
            ap = ap.bitcast(BF16)
        tot = 1
        for s in shape[1:]:
            tot *= s
        ap = ap[:, 0:tot]
        if len(shape) == 3:
            ap = ap.rearrange("p (a b) -> p a b", a=shape[1])
        elif len(shape) == 4:
            ap = ap.rearrange("p (a b c) -> p a b c", a=shape[1], b=shape[2])
        elif len(shape) == 5:
            ap = ap.rearrange("p (a b c d) -> p a b c d", a=shape[1], b=shape[2], c=shape[3])
        if shape[0] < 128:
            ap = ap[0:shape[0]]
        tv = TV(ap, Buf(name))
        tv.raw = raw
        return tv

    def alias(self, tv, shape, dt=F32, off=0):
        esz = 2 if dt == BF16 else 4
        tot = 1
        for s_ in shape[1:]:
            tot *= s_
        nw = (tot * esz + 3) // 4
        ap = tv.raw[:, off:off + nw]
        assert off + nw <= tv.raw.shape[1], "alias too large"
        if dt == BF16:
            ap = ap.bitcast(BF16)
        ap = ap[:, 0:tot]
        if len(shape) == 3:
            ap = ap.rearrange("p (a b) -> p a b", a=shape[1])
        elif len(shape) == 4:
            ap = ap.rearrange("p (a b c) -> p a b c", a=shape[1], b=shape[2])
        if shape[0] < 128:
            ap = ap[0:shape[0]]
        return TV(ap, tv.buf)

    def barrier(self):
        o = self.op("dve", lambda e: e.memset(self._bar_ap, 0.0), [], [])
        o.writes = [self.phase]

    def op(self, eng, fn, reads, writes, dma=False):
        o = Op()
        o.eng = eng
        o.fn = fn
        o.reads = [t.buf for t in reads if isinstance(t, TV) and t.buf is not None]
        o.writes = [t.buf for t in writes if isinstance(t, TV) and t.buf is not None]
        if getattr(self, "phase", None) is not None:
            o.reads.append(self.phase)
        o.dma = dma
        o.idx = len(self.ops)
        self.ops.append(o)
        return o

    @staticmethod
    def _a(x):
        return x.ap if isinstance(x, TV) else x

    def mm(self, out, lhsT, rhs, start=True, stop=True):
        a = self._a
        self.op("pe", lambda e: e.matmul(a(out), lhsT=a(lhsT), rhs=a(rhs), start=start, stop=stop), [lhsT, rhs], [out])

    def tr(self, out, in_, ident):
        a = self._a
        self.op("pe", lambda e: e.transpose(out=a(out), in_=a(in_), identity=a(ident)), [in_, ident], [out])

    def act(self, out, in_, func, bias=None, scale=None, accum=None, eng="act"):
        a = self._a
        kw = {}
        if bias is not None:
            kw["bias"] = a(bias)
        if scale is not None:
            kw["scale"] = a(scale)
        if accum is not None:
            kw["accum_out"] = a(accum)
        self.op(eng, lambda e: e.activation(out=a(out), in_=a(in_), func=func, **kw), [in_, bias, scale], [out, accum])

    def tt(self, eng, out, in0, in1, op):
        a = self._a
        self.op(eng, lambda e: e.tensor_tensor(out=a(out), in0=a(in0), in1=a(in1), op=op), [in0, in1], [out])

    def ts(self, eng, out, in0, s1, s2=None, op0=ALU.mult, op1=None, accum=None):
        a = self._a
        kw = {}
        if op1 is not None:
            kw["op1"] = op1
        if accum is not None:
            kw["accum_out"] = a(accum)
        self.op(eng, lambda e: e.tensor_scalar(out=a(out), in0=a(in0), scalar1=a(s1), scalar2=a(s2), op0=op0, **kw),
                [in0, s1, s2], [out, accum])

    def stt(self, eng, out, in0, scalar, in1, op0, op1):
        a = self._a
        self.op(eng, lambda e: e.scalar_tensor_tensor(out=a(out), in0=a(in0), scalar=a(scalar), in1=a(in1), op0=op0, op1=op1),
                [in0, scalar, in1], [out])

    def copy(self, eng, out, in_):
        a = self._a
        if eng == "act":
            self.op(eng, lambda e: e.copy(out=a(out), in_=a(in_)), [in_], [out])
        else:
            self.op(eng, lambda e: e.tensor_copy(out=a(out), in_=a(in_)), [in_], [out])

    def red(self, eng, out, in_, op=ALU.add, axis=AX.X):
        a = self._a
        self.op(eng, lambda e: e.tensor_reduce(out=a(out), in_=a(in_), axis=axis, op=op), [in_], [out])

    def recip(self, out, in_):
        a = self._a
        self.op("dve", lambda e: e.reciprocal(out=a(out), in_=a(in_)), [in_], [out])

    def memset(self, eng, out, val):
        a = self._a
        self.op(eng, lambda e: e.memset(a(out), val), [], [out])

    def scan(self, out, d0, d1, init=0.0, op0=ALU.mult, op1=ALU.add):
        a = self._a
        self.op("dve", lambda e: e.tensor_tensor_scan(out=a(out), data0=a(d0), data1=a(d1), initial=init, op0=op0, op1=op1),
                [d0, d1], [out])

    def aselect(self, out, in_, pattern, cmp, fill, base=0, cm=0):
        a = self._a
        self.op("pool", lambda e: e.affine_select(out=a(out), in_=a(in_), pattern=pattern, compare_op=cmp, fill=fill,
                                                   base=base, channel_multiplier=cm), [in_], [out])

    def dma(self, q, out, in_, noncontig=False):
        a = self._a
        nc = self.nc

        def fn(e):
            if noncontig:
                with nc.allow_non_contiguous_dma(reason="small strided"):
                    return e.dma_start(out=a(out), in_=a(in_))
            return e.dma_start(out=a(out), in_=a(in_))

        o = self.op(q, fn, [in_], [out], dma=True)
        return o

    def finish(self):
        nc = self.nc
        ops = self.ops
        for o in ops:
            deps = set()
            for b in o.reads:
                if b.last_w is not None:
                    deps.add(b.last_w)
            for b in o.writes:
                if b.last_w is not None:
                    deps.add(b.last_w)
                deps.update(b.readers)
            deps.discard(o)
            o.deps = deps
            for b in o.reads:
                b.readers.append(o)
            for b in o.writes:
                b.last_w = o
                b.readers = []
            if o.dma:
                cand = [b for b in (o.writes + o.reads) if b is not self.phase]
                sb = cand[0]
                o.sem = sb
                sb.cnt += 16
                o.val = sb.cnt
        for o in ops:
            nd = set()
            for d in o.deps:
                if o.eng == "pe" and (not o.dma) and d.eng == "pe" and (not d.dma):
                    continue
                if o.dma and d.dma and o.sem is d.sem:
                    continue
                nd.add(d)
            o.deps = nd
        for o in ops:
            o.sig = False
        for o in ops:
            for d in o.deps:
                d.sig = True
        cnt = {e: 0 for e in ENGS}
        for o in ops:
            if o.dma:
                continue
            if o.sig:
                cnt[o.eng] += 1
                o.ticket = cnt[o.eng]
        esem = {e: self.enter(nc.semaphore("es_" + e)) for e in ENGS}
        dsems = {}
        for o in ops:
            if o.dma and o.sem not in dsems:
                dsems[o.sem] = self.enter(nc.semaphore("ds_%d" % len(dsems)))
        self.n_dsems = len(dsems)
        final_waits = {}
        for o in ops:
            if o.dma:
                final_waits[dsems[o.sem]] = max(final_waits.get(dsems[o.sem], 0), o.val)

        per = {e: [o for o in ops if o.eng == e] for e in ENGS}

        def emit(ename, e):
            waited = {}
            for o in per[ename]:
                need = {}
                for d in o.deps:
                    if d.dma:
                        s, v = dsems[d.sem], d.val
                    else:
                        s, v = esem[d.eng], d.ticket
                    if need.get(s, 0) < v:
                        need[s] = v
                for s, v in need.items():
                    if waited.get(s, 0) < v:
                        e.wait_ge(s, v)
                        waited[s] = v
                ins = o.fn(e)
                if o.dma:
                    ins.then_inc(dsems[o.sem], 16)
                elif o.sig:
                    ins.then_inc(esem[ename], 1)
            if ename == "sp":
                for s, v in final_waits.items():
                    e.wait_ge(s, v)

        block = self.enter(nc.Block())

        @block.tensor
        def _(e):
            emit("pe", e)

        @block.scalar
        def _(e):
            emit("act", e)

        @block.vector
        def _(e):
            emit("dve", e)

        @block.gpsimd
        def _(e):
            emit("pool", e)

        @block.sync
        def _(e):
            emit("sp", e)

        for cm in reversed(self.ctx):
            cm.__exit__(None, None, None)
        self.ctx = []
        return nc

import math
import os
CUT = int(os.environ.get('KCUT', '0'))

P = 128
TT = 128
NT = int(os.environ.get('KNT', '16'))
NCH = TT // 128
NSB = 16
EPS = 1e-6
W_NAMES = ["norm_mix_g", "w_in", "tshift_mu", "gm_ln_g", "gm_ln_b", "gm_ws", "gm_bs", "rw_w0", "rw_w2", "rw_a0",
           "rw_a2", "rw_g2", "rw_kk", "rw_ka", "rw_rk", "rw_ln_g", "rw_ln_b", "w_out", "norm_x_g", "norm_mem_g",
           "w_xq", "w_xk", "w_xv", "w_xo", "norm_ffn_g", "w_up", "w_down", "final_g"]
W_SHAPES = {"norm_mix_g": [1024], "w_in": [1024, 2816], "tshift_mu": [1792], "gm_ln_g": [512], "gm_ln_b": [512],
            "gm_ws": [4, 128, 128], "gm_bs": [4, 128], "rw_w0": [512], "rw_w2": [64, 512], "rw_a0": [512],
            "rw_a2": [64, 512], "rw_g2": [128, 512], "rw_kk": [512], "rw_ka": [512], "rw_rk": [512],
            "rw_ln_g": [512], "rw_ln_b": [512], "w_out": [1024, 1024], "norm_x_g": [1024], "norm_mem_g": [1024],
            "w_xq": [1024, 1024], "w_xk": [1024, 1024], "w_xv": [1024, 1024], "w_xo": [1024, 1024],
            "norm_ffn_g": [1024], "w_up": [1024, 4096], "w_down": [4096, 1024], "final_g": [1024]}
OUT_SHAPES = {"y_p": [2048, 1024], "y_s": [16, 1024], "mk_o": [256, 1024], "mv_o": [256, 1024], "shp_o": [1792],
              "wkvp_o": [8, 64, 64], "cvp_o": [128, 512], "shs_o": [16, 1792], "wkvs_o": [16, 8, 64, 64],
              "cvs_o": [16, 512]}


def build(taps=(), stages=99, arena_kb=207):
    k = KB()
    nc = k.nc
    D = {}
    D["xp"] = k.dram("xp", [2048, 1024])
    D["xs"] = k.dram("xs", [16, 1024])
    D["mem"] = k.dram("mem", [256, 1024])
    D["ck"] = k.dram("ck", [16, 256, 1024])
    D["cv"] = k.dram("cv", [16, 256, 1024])
    D["ssh"] = k.dram("ssh", [16, 1792])
    D["swkv"] = k.dram("swkv", [16, 8, 64, 64])
    for n in W_NAMES:
        D[n] = k.dram(n, W_SHAPES[n])
    O = {n: k.dram(n, s, kind="ExternalOutput") for n, s in OUT_SHAPES.items()}
    tapd = {}

    pb = [k.ps("pb%d" % i, [128, 512]) for i in range(8)]
    k.arena_init(arena_kb * 1024)
    A = k.alloc

    def tap(name, tv, shape):
        if name not in taps:
            return
        t = k.dram("tap_" + name, list(shape), dt=tv.ap.dtype, kind="ExternalOutput")
        tapd[name] = t
        k.dma("sp", t, tv)

    big_i = [0]

    def nextbig():
        big_i[0] += 1
        return pb[6 + big_i[0] % 2]

    ident = A("ident", [128, 128], BF16)
    identf = A("identf", [128, 128], F32)
    for t in (ident, identf):
        k.memset("pool", t, 0.0)
        k.aselect(t, t, [[-1, 128]], ALU.not_equal, 1.0, base=0, cm=1)
    mask2 = A("mask2", [128, 2, 128], F32)
    k.memset("pool", mask2, 1.0)
    k.aselect(mask2[:, 0, :], mask2[:, 0, :], [[1, 128]], ALU.is_gt, 0.0, base=0, cm=-1)
    k.aselect(mask2[:, 1, :], mask2[:, 1, :], [[1, 128]], ALU.is_ge, 0.0, base=0, cm=-1)
    maskN = A("maskN", [128, 128], F32)
    k.memset("pool", maskN, 1.0)
    k.aselect(maskN, maskN, [[-1, 128]], ALU.is_gt, 0.0, base=0, cm=1)
    segm = A("segm", [128, TT], F32)
    k.memset("pool", segm, 1.0)
    k.memset("pool", segm.re("p (j t) -> p j t", t=128)[:, :, 0:1], 0.0)
    blk2 = A("blk2", [128, 128], F32)
    k.memset("pool", blk2, 1.0)
    k.memset("pool", blk2[0:64, 64:128], 0.0)
    k.memset("pool", blk2[64:128, 0:64], 0.0)
    hsel = A("hsel", [128, 2], BF16)
    k.memset("pool", hsel, 0.0)
    k.memset("pool", hsel[0:64, 0:1], 1.0)
    k.memset("pool", hsel[64:128, 1:2], 1.0)

    prm = A("prm", [128, 80], F32)
    G_MIX, G_X, G_FFN, G_MEM = 0, 1, 2, 3
    for i, n in enumerate(["norm_mix_g", "norm_x_g", "norm_ffn_g", "norm_mem_g"]):
        k.dma("sp", prm[:, i * 8:(i + 1) * 8], D[n].re("(c p) -> p c", p=128), noncontig=True)
    k.dma("sp", prm[:, 40:54], D["tshift_mu"].re("(c p) -> p c", p=128), noncontig=True)
    for i, n in enumerate(["rw_w0", "rw_a0", "rw_kk", "rw_ka", "rw_rk"]):
        k.dma("sp", prm[:, 54 + 4 * i:58 + 4 * i], D[n].re("(c p) -> p c", p=128), noncontig=True)
    gT = lambda gi: prm[:, gi * 8:(gi + 1) * 8]
    mu = prm[:, 40:54]
    pw0, pa0, pkk, pka, prk = (prm[:, 54 + 4 * i:58 + 4 * i] for i in range(5))
    omm = A("omm", [128, 14], F32)
    k.ts("dve", omm, mu, -1.0, 1.0, op0=ALU.mult, op1=ALU.add)

    bct = A("bct", [128, 5, 512], F32)
    for i, n in enumerate(["gm_ln_g", "gm_ln_b", "rw_ln_g", "rw_ln_b"]):
        k.dma("sp", bct[:, i, :], TV(D[n].ap.partition_broadcast(128), None))
    k.dma("sp", bct[:, 4, :], TV(D["gm_bs"].ap.rearrange("h t -> (h t)").partition_broadcast(128), None))

    w2a2 = A("w2a2", [128, 512], BF16)
    k.dma("pool", w2a2[0:64, :], D["rw_w2"])
    k.dma("pool", w2a2[64:128, :], D["rw_a2"])
    g2 = A("g2", [128, 512], BF16)
    k.dma("pool", g2, D["rw_g2"])

    xr = [A("xr%d" % j, [128, 1024], F32) for j in range(16)]
    xpv = D["xp"].re("(j p) d -> j p d", p=128)
    for j in range(2):
        k.dma("sp", xr[j], xpv[j])

    xsb = [A("xsb%d" % i, [128, 1024], BF16) for i in range(1)]
    sqs = A("sqs", [128, 512], F32)
    junk = TV(sqs.ap.bitcast(BF16), sqs.buf)

    ss = A("ss", [128, 8], F32)
    rs = A("rs", [128, 8], F32)
    lnst = A("lnst", [128, 48], F32)

    def rms_A(xtvs, pt, xbs, rs_, junk_=None, ss_=None):
        n = len(xtvs)
        junk_ = junk if junk_ is None else junk_
        ss_ = ss if ss_ is None else ss_
        k.memset("dve", ss_, 0.0)
        for j, x in enumerate(xtvs):
            k.act(junk_[:pt], x, AF.Square, accum=ss_[:pt, j:j + 1])
        k.act(rs_[:pt, 0:n], ss_[:pt, 0:n], AF.Sqrt, bias=EPS, scale=1.0 / 1024)
        k.recip(rs_[:pt, 0:n], rs_[:pt, 0:n])
        for j, x in enumerate(xtvs):
            k.act(xbs[j][:pt], x, AF.Copy, scale=rs_[:pt, j:j + 1])

    def rms_B(n, pt, gi, hT, toff, xbs):
        for j in range(n):
            ptv = nextbig().bitcast(BF16).re("p (c t) -> p c t", c=8)
            for c in range(8):
                k.tr(ptv[:, c, 0:pt], xbs[j][:pt, c * 128:(c + 1) * 128], ident[:pt, :pt])
            k.tt("dve", hT[:, :, toff + j * pt:toff + (j + 1) * pt], ptv[:, :, 0:pt],
                 gT(gi).un(2).bc([128, 8, pt]), ALU.mult)

    def rms_T(xtvs, pt, gi, hT, toff):
        n = len(xtvs)
        k.memset("dve", ss, 0.0)
        for j, x in enumerate(xtvs):
            k.act(junk[:pt], x, AF.Square, accum=ss[:pt, j:j + 1])
        k.act(rs[:pt, 0:n], ss[:pt, 0:n], AF.Sqrt, bias=EPS, scale=1.0 / 1024)
        k.recip(rs[:pt, 0:n], rs[:pt, 0:n])
        for j, x in enumerate(xtvs):
            xb = xsb[0]
            k.act(xb[:pt], x, AF.Copy, scale=rs[:pt, j:j + 1])
            ptv = nextbig().bitcast(BF16).re("p (c t) -> p c t", c=8)
            for c in range(8):
                k.tr(ptv[:, c, 0:pt], xb[:pt, c * 128:(c + 1) * 128], ident[:pt, :pt])
            k.tt("dve", hT[:, :, toff + j * pt:toff + (j + 1) * pt], ptv[:, :, 0:pt],
                 gT(gi).un(2).bc([128, 8, pt]), ALU.mult)
        return rs

    def ln_heads(pt, x, H, Dh, eps, gb, bb, out, scr):
        x3 = x.re("p (h d) -> p h d", h=H)
        s3 = scr.re("p (h d) -> p h d", h=H)
        st = lnst
        s1, s2, mean, msq, var = (st[:pt, i * 8:i * 8 + H] for i in range(5))
        k.red("dve", s1, x3[:pt])
        k.act(scr[:pt], x[:pt], AF.Square)
        k.red("dve", s2, s3[:pt])
        k.ts("dve", mean, s1, 1.0 / Dh, None, op0=ALU.mult)
        k.tt("dve", msq, mean, mean, ALU.mult)
        k.stt("dve", var, s2, 1.0 / Dh, msq, op0=ALU.mult, op1=ALU.subtract)
        k.act(var, var, AF.Sqrt, bias=eps, scale=1.0)
        k.recip(var, var)
        k.tt("dve", s3[:pt], x3[:pt], mean.un(2).bc([pt, H, Dh]), ALU.subtract)
        k.tt("dve", s3[:pt], s3[:pt], var.un(2).bc([pt, H, Dh]), ALU.mult)
        k.tt("dve", scr[:pt], scr[:pt], gb[:pt], ALU.mult)
        k.tt("dve", out[:pt], scr[:pt], bb[:pt], ALU.add)


    B = NSB
    xsr = A("xsr", [B, 1024], F32)
    k.dma("sp", xsr, D["xs"])

    def make_sel(dt=F32):
        sel_ = A("sel", [B, B, 128], dt)
        k.memset("pool", sel_, 1.0)
        k.aselect(sel_, sel_, [[-1, B], [0, 128]], ALU.is_equal, 0.0, base=0, cm=1)
        return sel_

    def sample_S1():
        mixTs = A("mixTs", [128, 8, B], BF16)
        nkk = A("s_kk", [B, 512], F32)
        dcy = A("s_d", [B, 512], F32)
        bvec = A("s_b", [B, 512], F32)
        kmods = A("s_k", [B, 512], F32)
        rtok = A("s_r", [B, 512], F32)
        vtok = A("s_v", [B, 512], F32)
        gtok = A("s_g", [B, 512], F32)
        bons = A("s_bon", [B, 8], F32)
        mS1x = k.aoff
        bcs = [A("bcs%d" % i, [B, 512], F32) for i in range(3)]

        def bload(slot, n):
            k.dma("sp", bcs[slot], TV(D[n].ap.partition_broadcast(B), None))
            return bcs[slot]
        w0b = bload(0, "rw_w0")
        a0b = bload(1, "rw_a0")
        kkb = bload(2, "rw_kk")
        hTs = A("hTs", [128, 8, B], BF16)
        ptok = A("ptok", [B, 2816], F32)
        sshb = [A("sshb%d" % i, [B, 512], F32) for i in range(2)]
        mubb = [A("mubb%d" % i, [B, 512], F32) for i in range(2)]
        ws0 = A("ws0", [B, 8], F32)
        k.dma("sp", ws0[:, 0:4], TV(D["gm_ws"].ap.rearrange("h t s -> h (t s)")[:, 0].partition_broadcast(B), None), noncontig=True)
        k.dma("sp", ws0[:, 4:8], TV(D["gm_bs"].ap[:, 0].partition_broadcast(B), None), noncontig=True)
        pbs = A("pbs", [B, 1792], F32)
        vns = A("vns", [B, 512], F32)
        oas = A("oas", [B, 512], BF16)
        tws = A("tws", [B, 256], BF16)
        twT = A("twT", [128, 2, B], BF16)
        s1 = A("s_t1", [B, 512], F32)
        s2 = A("s_t2", [B, 512], F32)
        s8 = A("s_s8", [B, 16], F32)

        rms_T([xsr], B, G_MIX, hTs, 0)
        for c0 in range(0, 2816, 512):
            n = min(512, 2816 - c0)
            ps = nextbig()
            for c in range(8):
                k.mm(ps[:B, 0:n], hTs[:, c, :], win[:, c, c0:c0 + n], start=(c == 0), stop=(c == 7))
            k.act(ptok[:, c0:c0 + n], ps[:B, 0:n], AF.Gelu if c0 < 1024 else AF.Copy)
        k.dma("sp", O["shs_o"], ptok[:, 1024:2816])
        ln_heads(B, ptok[:, 512:1024], 4, 128, 1e-5, bct[:, 0, :], bct[:, 1, :], vns, sqs)
        k.dma("sp", O["cvs_o"], vns)
        k.tt("pool", s1.re("p (h d) -> p h d", h=4), vns.re("p (h d) -> p h d", h=4), ws0[:, 0:4].un(2).bc([B, 4, 128]), ALU.mult)
        k.tt("pool", s1.re("p (h d) -> p h d", h=4), s1.re("p (h d) -> p h d", h=4), ws0[:, 4:8].un(2).bc([B, 4, 128]), ALU.add)
        k.tt("pool", oas, s1, ptok[:, 0:512], ALU.mult)
        ptv = nextbig().bitcast(BF16).re("p (c t) -> p c t", c=8)
        for h in range(4):
            k.tr(ptv[:, h, 0:B], oas[:, h * 128:(h + 1) * 128], ident[:B, :B])
        k.copy("dve", mixTs[:, 0:4, :], ptv[:, 0:4, 0:B])
        for bi, c0 in enumerate(range(0, 1792, 512)):
            n = min(512, 1792 - c0)
            sb_, mb_ = sshb[bi % 2][:, 0:n], mubb[bi % 2][:, 0:n]
            k.dma("sp", sb_, D["ssh"][:, c0:c0 + n])
            k.dma("sp", mb_, TV(D["tshift_mu"].ap[c0:c0 + n].partition_broadcast(B), None))
            pp = ptok[:, 1024 + c0:1024 + c0 + n]
            k.tt("dve", pbs[:, c0:c0 + n], sb_, pp, ALU.subtract)
            k.tt("dve", pbs[:, c0:c0 + n], pbs[:, c0:c0 + n], mb_, ALU.mult)
            k.tt("dve", pbs[:, c0:c0 + n], pbs[:, c0:c0 + n], pp, ALU.add)
        tap("s_pbs", pbs, [B, 1792])
        r_, k_, v_ = pbs[:, 0:512], pbs[:, 512:1024], pbs[:, 1024:1536]
        k.copy("dve", rtok, r_)
        k.copy("dve", vtok, v_)
        k.act(tws[:, 0:64], pbs[:, 1536:1600], AF.Tanh)
        k.copy("dve", tws[:, 64:128], pbs[:, 1600:1664])
        k.act(tws[:, 128:256], pbs[:, 1664:1792], AF.Sigmoid)
        ptv = nextbig().bitcast(BF16).re("p (c t) -> p c t", c=8)
        for i in range(2):
            k.tr(ptv[:, i, 0:B], tws[:, i * 128:(i + 1) * 128], ident[:B, :B])
        k.copy("dve", twT, ptv[:, 0:2, 0:B])
        ps = nextbig()
        k.mm(ps[:B, :], twT[0:64, 0, :], w2a2[0:64, :])
        k.tt("dve", s1, ps[:B, :], w0b, ALU.add)
        k.act(s1, s1, AF.Sigmoid)
        k.ts("dve", s1, s1, -math.exp(-0.5), None, op0=ALU.mult)
        k.act(dcy, s1, AF.Exp)
        ps = nextbig()
        k.mm(ps[:B, :], twT[64:128, 0, :], w2a2[64:128, :])
        k.tt("dve", s2, ps[:B, :], a0b, ALU.add)
        k.act(s2, s2, AF.Sigmoid)
        ps = nextbig()
        k.mm(ps[:B, :], twT[:, 1, :], g2)
        k.copy("dve", gtok, ps[:B, :])
        k.tt("pool", nkk, k_, kkb, ALU.mult)
        k.tt("pool", s1, nkk, nkk, ALU.mult)
        k.red("dve", s8[:, 0:8], s1.re("p (h d) -> p h d", h=8))
        k.act(s8[:, 0:8], s8[:, 0:8], AF.Sqrt)
        k.ts("dve", s8[:, 0:8], s8[:, 0:8], 1e-12, None, op0=ALU.max)
        k.recip(s8[:, 0:8], s8[:, 0:8])
        k.tt("pool", nkk.re("p (h d) -> p h d", h=8), nkk.re("p (h d) -> p h d", h=8), s8[:, 0:8].un(2).bc([B, 8, 64]), ALU.mult)
        k.tt("pool", bvec, nkk, s2, ALU.mult)
        kab = bload(0, "rw_ka")
        rkb = bload(1, "rw_rk")
        k.ts("dve", s1, s2, -1.0, None, op0=ALU.add)
        k.tt("dve", s1, s1, kab, ALU.mult)
        k.ts("dve", s1, s1, 1.0, None, op0=ALU.add)
        k.tt("dve", kmods, k_, s1, ALU.mult)
        k.tt("dve", s1, r_, kmods, ALU.mult)
        k.tt("dve", s1, s1, rkb, ALU.mult)
        k.red("dve", bons, s1.re("p (h d) -> p h d", h=8))
        tap("s_kk", nkk, [B, 512])
        tap("s_d", dcy, [B, 512])
        k.barrier()
        k.aoff = mS1x
        wq_pf = k.alias(win, [128, 8, 1024], BF16, off=0)
        wo_pf = k.alias(win, [128, 8, 1024], BF16, off=4096)
        for c in range(8):
            k.dma("pool", wq_pf[:, c, :], D["w_xq"].re("(c p) n -> c p n", p=128)[c])
        for c in range(8):
            k.dma("pool", wo_pf[:, c, :], D["w_xo"].re("(c p) n -> c p n", p=128)[c])
        sel = make_sel(BF16)
        St = [A("St%d" % i, [64, 8, 64], F32) for i in range(2)]
        Sn = [A("Sn%d" % i, [64, 8, 64], F32) for i in range(2)]
        tS = A("tS", [64, 8, 64], F32)
        vTs = A("vTs", [64, B, 8], F32)
        yTs = A("yTs", [64, B, 8], F32)
        sa = A("sa", [64, 8], F32)
        Ys = A("Ys", [B, 512], F32)
        yns = A("yns", [B, 512], F32)
        obs = A("obs", [B, 512], BF16)
        psv = [pb[0], pb[1]]
        for h in range(8):
            k.tr(psv[h % 2][0:64, (h // 2) * B:(h // 2 + 1) * B], vtok[:, h * 64:(h + 1) * 64], identf[:B, :B])
        for e in range(2):
            k.copy("dve", vTs.re("p b h -> p h b")[:, e::2, :], psv[e][0:64, 0:4 * B].re("p (q b) -> p q b", q=4))
        swv = D["swkv"].re("b h v k -> b v h k")
        wsv = O["wkvs_o"].re("b h v k -> b v h k")
        vecb = A("vecb", [B, 5, 512], BF16)
        for i, vec in enumerate((nkk, dcy, bvec, kmods, rtok)):
            k.copy("dve", vecb[:, i, :], vec)
        vecs = [vecb[:, i, :] for i in range(5)]
        bcb = [[A("bcb%d_%d" % (j, i), [64, 512], F32) for i in range(5)] for j in range(1)]
        bcb2 = [A("bcbx_%d" % i, [64, 512], F32) for i in range(2, 5)]
        tS2 = A("tS2", [64, 8, 64], F32)
        for b in range(B):
            st, sn = St[b % 2], Sn[b % 2]
            k.dma("sp", st, swv[b])
            pv_ = [pb[i] for i in range(5)]
            bc_ = bcb[0] if b % 2 == 0 else bcb[0][0:2] + bcb2
            for i, vec in enumerate(vecs):
                k.mm(pv_[i][0:64, :], sel[:, b, 0:64], vec)
                k.act(bc_[i], pv_[i][0:64, :], AF.Copy)
            f2 = lambda t: t.re("p h k -> p (h k)")
            r3 = lambda t: t.re("p (h k) -> p h k", h=8)
            k.tt("dve", f2(tS), f2(st), bc_[0], ALU.mult)
            k.red("dve", sa, tS)
            k.tt("pool", f2(sn), f2(st), bc_[1], ALU.mult)
            k.tt("dve", tS2, r3(bc_[2]), sa.un(2).bc([64, 8, 64]), ALU.mult)
            k.tt("pool", sn, sn, tS2, ALU.subtract)
            k.tt("dve", tS, r3(bc_[3]), vTs[:, b, :].un(2).bc([64, 8, 64]), ALU.mult)
            k.tt("pool", sn, sn, tS, ALU.add)
            k.dma("pool", wsv[b], sn)
            k.tt("dve", f2(tS2), f2(sn), bc_[4], ALU.mult)
            k.red("dve", yTs[:, b, :], tS2)
        psy = [pb[5], pb[6]]
        for h in range(8):
            k.tr(psy[h % 2][:B, (h // 2) * 64:(h // 2 + 1) * 64], yTs[:, :, h], identf[0:64, 0:64])
        for e in range(2):
            k.copy("dve", Ys.re("p (q e v) -> p q e v", q=4, e=2)[:, :, e, :], psy[e][:B, 0:256].re("p (q v) -> p q v", q=4))
        tap("s_Y", Ys, [B, 512])
        ln_heads(B, Ys, 8, 64, 64e-5, bct[:, 2, :], bct[:, 3, :], yns, sqs)
        k.tt("dve", sqs[:B].re("p (h v) -> p h v", h=8), vtok.re("p (h v) -> p h v", h=8), bons.un(2).bc([B, 8, 64]), ALU.mult)
        k.tt("dve", yns, yns, sqs[:B], ALU.add)
        k.tt("dve", obs, yns, gtok, ALU.mult)
        ptv = nextbig().bitcast(BF16).re("p (c t) -> p c t", c=8)
        for p in range(4):
            k.tr(ptv[:, p, 0:B], obs[:, p * 128:(p + 1) * 128], ident[:B, :B])
        k.copy("dve", mixTs[:, 4:8, :], ptv[:, 0:4, 0:B])
        for hf in range(2):
            ps = nextbig()
            for c in range(8):
                k.mm(ps[:B, :], mixTs[:, c, :], wout[:, c, hf * 512:(hf + 1) * 512], start=(c == 0), stop=(c == 7))
            xs_ = xsr[:, hf * 512:(hf + 1) * 512]
            k.tt("dve", xs_, ps[:B, :], xs_, ALU.add)
        tap("s_x1", xsr, [B, 1024])


    def sample_S2():
        sel = make_sel(BF16)
        hT2 = A("hTs2", [128, 8, B], BF16)
        qtok = A("qtok", [B, 1024], BF16)
        KV = [A("KV%d" % i, [128, 2, 1024], F32) for i in range(5)]
        sall = A("sall", [128, 2, B, 4], F32)
        prod = [A("prod%d" % i, [128, 512], F32) for i in range(2)]
        qbs = [A("qbs%d" % i, [128, 1024], F32) for i in range(2)]
        Ps = A("Ps", [64, 256], F32)
        PTs = A("PTs", [128, 2, 64], BF16)
        vbf = [A("vbf%d" % i, [128, 2, 1024], BF16) for i in range(2)]
        oTs = A("oTs", [128, 8, B], BF16)
        sm2 = A("sm2", [64, 8], F32)
        rms_T([xsr], B, G_X, hT2, 0)
        for hf in range(2):
            ps = nextbig()
            for c in range(8):
                k.mm(ps[:B, :], hT2[:, c, :], wq[:, c, hf * 512:(hf + 1) * 512], start=(c == 0), stop=(c == 7))
            k.ts("dve", qtok[:, hf * 512:(hf + 1) * 512], ps[:B, :], 0.0625, None, op0=ALU.mult)
        ckv = D["ck"].re("b (mt m) d -> b m mt d", mt=2)
        cvv = D["cv"].re("b (mt m) d -> b m mt d", mt=2)
        for b in range(B):
            kt = KV[b % 5]
            k.dma("sp", kt, ckv[b])
            qb = [pb[0], pb[1]]
            for hf in range(2):
                k.mm(qb[hf], sel[:, b, :], qtok[:, hf * 512:(hf + 1) * 512])
                k.act(qbs[b % 2][:, hf * 512:(hf + 1) * 512], qb[hf], AF.Copy)
            for mt in range(2):
                for hf in range(2):
                    pr = prod[(mt * 2 + hf) % 2]
                    k.tt("pool" if (mt * 2 + hf) % 2 == 0 else "dve", pr, kt[:, mt, hf * 512:(hf + 1) * 512], qbs[b % 2][:, hf * 512:(hf + 1) * 512], ALU.mult)
                    k.red("dve", sall[:, mt, b, 2 * hf:2 * hf + 2], pr.re("p (h d) -> p h d", h=2))
        pst = pb[2]
        for mt in range(2):
            k.tr(pst[0:64, mt * 128:(mt + 1) * 128], sall[:, mt].re("p b h -> p (b h)"), identf)
        mx, nmx, ssum, rinv = (sm2[:, i:i + 1] for i in range(4))
        k.red("dve", mx, pst[0:64, 0:256], op=ALU.max)
        k.ts("dve", nmx, mx, -1.0, None, op0=ALU.mult)
        k.memset("dve", ssum, 0.0)
        k.act(Ps, pst[0:64, 0:256], AF.Exp, bias=nmx, scale=1.0, accum=ssum)
        k.recip(rinv, ssum)
        k.ts("dve", Ps, Ps, rinv, None, op0=ALU.mult)
        psp = pb[3]
        for mt in range(2):
            k.tr(psp[:, mt * 64:(mt + 1) * 64], Ps[:, mt * 128:(mt + 1) * 128], identf[0:64, 0:64])
        k.copy("dve", PTs.re("p a b -> p (a b)"), psp[:, 0:128])
        pso = pb[4]
        for b in range(B):
            vt32 = KV[(b + 1) % 5]
            k.dma("sp", vt32, cvv[b])
            vt = vbf[b % 2]
            k.act(vt[:, 0, :], vt32[:, 0, :], AF.Copy)
            k.copy("dve", vt[:, 1, :], vt32[:, 1, :])
            for c in range(8):
                h = c // 2
                for mt in range(2):
                    k.mm(pso[:, c * B + b:c * B + b + 1], vt[:, mt, c * 128:(c + 1) * 128], PTs[:, mt, b * 4 + h:b * 4 + h + 1], start=(mt == 0), stop=(mt == 1))
        k.copy("dve", oTs, pso[:, 0:8 * B].re("p (c b) -> p c b", c=8))
        tap("s_oT", oTs, [128, 8, B])
        for hf in range(2):
            ps = nextbig()
            for c in range(8):
                k.mm(ps[:B, :], oTs[:, c, :], wo[:, c, hf * 512:(hf + 1) * 512], start=(c == 0), stop=(c == 7))
            xs_ = xsr[:, hf * 512:(hf + 1) * 512]
            k.tt("dve", xs_, ps[:B, :], xs_, ALU.add)
        tap("s_x2", xsr, [B, 1024])


    def sample_ffn_prep():
        hT3 = A("hTs3", [128, 8, B], BF16)
        hidS = A("hidS", [128, 4, B], BF16)
        rls = A("rls", [128, B], F32)
        rms_T([xsr], B, G_FFN, hT3, 0)
        return hT3, hidS, rls

    def sample_ffn_block(st, wu_, wd_):
        hT3, hidS, rls = st
        for hc in range(4):
            ps = pb[hc % 4]
            for c in range(8):
                k.mm(ps[:, 0:B], wu_[:, c, hc * 128:(hc + 1) * 128], hT3[:, c, :], start=(c == 0), stop=(c == 7))
            k.act(rls, ps[:, 0:B], AF.Relu)
            k.act(hidS[:, hc, :], rls, AF.Square)
        for hf in range(2):
            ps = pb[4 + hf]
            for hc in range(4):
                k.mm(ps[:B, :], hidS[:, hc, :], wd_[:, hc, hf * 512:(hf + 1) * 512], start=(hc == 0), stop=(hc == 3))
            xs_ = xsr[:, hf * 512:(hf + 1) * 512]
            k.tt("dve", xs_, ps[:B, :], xs_, ALU.add)

    def sample_final(fgb_):
        yos = A("yos", [B, 1024], F32)
        k.memset("dve", ss, 0.0)
        k.act(junk[:B], xsr, AF.Square, accum=ss[:B, 0:1])
        k.act(rs[:B, 0:1], ss[:B, 0:1], AF.Sqrt, bias=EPS, scale=1.0 / 1024)
        k.recip(rs[:B, 0:1], rs[:B, 0:1])
        k.stt("dve", yos, xsr, rs[:B, 0:1], fgb_[:B], op0=ALU.mult, op1=ALU.mult)
        k.dma("sp", O["y_s"], yos)

    m0 = k.aoff
    WT = A("WT", [128, 4, 128], BF16)
    m0b = k.aoff
    win = A("win", [128, 8, 2816], BF16)
    wout = A("wout", [128, 8, 1024], BF16)
    for c in range(8):
        k.dma("pool", win[:, c, :], D["w_in"].re("(c p) n -> c p n", p=128)[c])
    for c in range(8):
        k.dma("pool", wout[:, c, :], D["w_out"].re("(c p) n -> c p n", p=128)[c])
    m_ws = k.aoff
    wsf = A("wsf", [128, 4, 128], F32)
    k.dma("sp", wsf, D["gm_ws"].re("h t s -> t h s"))
    wsb = A("wsb", [128, 4, 128], BF16)
    for h in range(4):
        k.aselect(wsf[:, h, :], wsf[:, h, :], [[-1, 128]], ALU.is_ge, 0.0, base=0, cm=1)
    k.copy("pool", wsb, wsf)
    ptv = nextbig().bitcast(BF16).re("p (c t) -> p c t", c=8)
    for h in range(4):
        k.tr(ptv[:, h, :], wsb[:, h, :], ident)
    k.copy("dve", WT, ptv[:, 0:4, :])
    k.barrier()
    k.aoff = m_ws

    m_w1 = k.aoff
    hT = A("hT", [128, 8, TT], BF16)
    uT = A("uT", [128, 4, TT], F32)
    vg = A("vg", [128, 512], F32)
    vnf = A("vnf", [128, 512], F32)
    vnb = A("vnb", [128, NCH, 512], BF16)
    mixTs_ = [A("mixT%d" % i, [128, 8, TT], BF16) for i in range(2)]
    tmpA = A("tmpA", [128, TT], F32)
    prevcol = A("prevcol", [128, 14], F32)
    k.memset("pool", prevcol, 0.0)
    twal = A("twal", [128, TT], BF16)
    sgl = A("sgl", [128, TT], BF16)
    PBx = A("PBx", [128, 2, 128], F32)
    tmpx = [A("tmpx%d" % i, [128, 4, 129], F32) for i in range(2)]
    dd = A("dd", [128, 4, 128], F32)
    I4 = A("I4", [128, 4, 128], BF16)
    k.copy("pool", I4, ident.un(1).bc([128, 4, 128]))
    segm4 = A("segm4", [128, 4, 128], F32)
    k.memset("pool", segm4, 1.0)
    k.memset("pool", segm4[:, :, 0:1], 0.0)
    AR = A("AR", [128, 4, NCH, 2, 128], BF16)
    BKT = A("BKT", [128, 4, 2, TT], BF16)
    VT = A("VT", [128, 4, TT], BF16)
    rkr = A("rkr", [128, 4, TT], BF16)
    PC = A("PC", [128, 4, NCH], F32)
    Btok = A("Btok", [128, 512], BF16)
    Ktok = A("Ktok", [128, 512], BF16)
    Vtok = A("Vtok", [128, 512], BF16)
    LrbT = A("LrbT", [128, 8, 128], BF16)
    LT2 = A("LT2", [128, 8, 256], BF16)
    Mb = [A("Mb%d" % g, [128, 4, 128], BF16) for g in range(2)]
    MTP = [A("MTP%d" % g, [128, 4, 2, 128], BF16) for g in range(2)]
    Pf = [A("Pf%d" % g, [128, 4, 128], F32) for g in range(2)]
    Xb = A("Xb", [128, 8, 64], BF16)
    Ub = A("Ub", [128, 8, 64], BF16)
    Hs = A("Hs", [128, 4, 64], F32)
    Hb = [A("Hb%d" % i, [128, 4, 64], BF16) for i in range(2)]
    tmpH = A("tmpH", [128, 4, 64], F32)
    PBr = k.alias(Pf[0], [128, 4, 128])
    PBk = k.alias(Pf[1], [128, 4, 128])
    PBv = k.alias(MTP[0], [128, 4, 128])
    sg = k.alias(MTP[1], [128, 4, 128])
    asig = k.alias(LT2, [128, 4, 128], off=0)
    cum = k.alias(LT2, [128, 4, 128], off=512)
    Ex = k.alias(LrbT, [128, 4, 128])
    kk = k.alias(vg, [128, 4, 128])
    kmod = k.alias(vnf, [128, 4, 128])
    t1 = k.alias(sqs, [128, 4, 128])
    t2 = k.alias(uT, [128, 4, 128])
    Ysb = vg
    yn = vnf
    bon = A("bon", [128, 8], F32)
    ob = A("ob", [128, 512], BF16)
    k.memset("pool", Hs, 0.0)
    k.memset("pool", Hb[0], 0.0)
    hb_i = [0]
    m2f = mask2.re("p a t -> p (a t)")

    pg_i = [0]

    def proj_group(c0_, n, out3):
        ps = nextbig()
        for i in range(n):
            m = c0_ + i
            for c in range(8):
                k.mm(ps[:, i * 128:(i + 1) * 128], win[:, c, 1024 + m * 128:1024 + (m + 1) * 128], hT[:, c, :], start=(c == 0), stop=(c == 7))
        tx = tmpx[pg_i[0] % 2]
        pg_i[0] += 1
        k.copy("dve", tx[:, 0:n, 0:1], prevcol[:, c0_:c0_ + n].un(2))
        k.act(tx[:, 0:n, 1:129], ps[:, 0:n * 128].re("p (c t) -> p c t", c=n), AF.Copy)
        k.copy("dve", prevcol[:, c0_:c0_ + n].un(2), tx[:, 0:n, 128:129])
        k.tt("dve", dd[:, 0:n, :], tx[:, 0:n, 0:128], tx[:, 0:n, 1:129], ALU.subtract)
        k.tt("dve", dd[:, 0:n, :], dd[:, 0:n, :], mu[:, c0_:c0_ + n].un(2).bc([128, n, 128]), ALU.mult)
        k.tt("dve", out3, dd[:, 0:n, :], tx[:, 0:n, 1:129], ALU.add)

    NFILL_A = int(os.environ.get('KFILLA', '0'))
    NFILL_B = int(os.environ.get('KFILLB', '0'))
    I4f = I4.re("p a t -> p (a t)")

    def pe_fill(n, bank):
        for _ in range(n):
            k.mm(bank, ident, I4f, start=True, stop=True)

    pending_wout = []

    def flush_wout():
        while pending_wout:
            T_, mx_ = pending_wout.pop(0)
            for hf in range(2):
                ps = nextbig()
                for c in range(8):
                    k.mm(ps, mx_[:, c, :], wout[:, c, hf * 512:(hf + 1) * 512], start=(c == 0), stop=(c == 7))
                xs_ = xr[T_][:, hf * 512:(hf + 1) * 512]
                k.tt("dve", xs_, ps, xs_, ALU.add)

    v2 = lambda tv: tv.re("p (j t) -> p j t", j=NCH)
    ntiles = NT if stages >= 3 else (2 if stages >= 2 else 1)
    junk2 = k.alias(tmpx[1], [128, 1024], BF16)
    rsN = A("rsN", [128, 8], F32)
    ssN = A("ssN", [128, 8], F32)
    pre_rms = [False]
    for T in range(ntiles):
        mixT = mixTs_[T % 2]
        for j_ in ((2, 3) if T == 0 else (T + 3,)):
            if j_ < 16:
                k.dma("sp", xr[j_], xpv[j_])
        if not pre_rms[0]:
            rms_T([xr[T]], 128, G_MIX, hT, 0)
        pre_rms[0] = False
        if T == 0:
            tap("hT", hT, [128, 8, TT])
        def mixA_u(ms):
            for m in ms:
                ps = nextbig()
                for c in range(8):
                    k.mm(ps[:, 0:TT], win[:, c, m * 128:(m + 1) * 128], hT[:, c, :], start=(c == 0), stop=(c == 7))
                k.act(uT[:, m, :], ps[:, 0:TT], AF.Gelu)

        def mixA_v():
            ps = nextbig()
            for c in range(8):
                k.mm(ps, hT[:, c, 0:128], win[:, c, 512:1024], start=(c == 0), stop=(c == 7))
            k.act(vg, ps, AF.Gelu)

        def mixA_ln():
            ln_heads(128, vg, 4, 128, 1e-5, bct[:, 0, :], bct[:, 1, :], vnf, sqs)
            k.act(vnb[:, 0, :], vnf, AF.Copy)
            if T == NT - 1:
                k.dma("sp", O["cvp_o"], vnf)
            if T == 0:
                tap("vnf", vnf, [128, 512])

        def mixA_gate(hs):
            for h in hs:
                ps = nextbig()
                k.mm(ps[:, 0:128], vnb[:, 0, h * 128:(h + 1) * 128], WT[:, h, :])
                k.tt("dve", tmpA, ps[:, 0:TT], bct[:, 4, h * 128:(h + 1) * 128], ALU.add)
                k.tt("dve", mixT[:, h, :], tmpA, uT[:, h, :], ALU.mult)

        mixA_sched = {0: lambda: mixA_u([0, 1]), 1: lambda: mixA_u([2, 3]), 2: mixA_v, 3: mixA_ln,
                      4: lambda: mixA_gate([0, 1]), 5: lambda: mixA_gate([2, 3])}
        if stages <= 1:
            for i_ in range(6):
                mixA_sched[i_]()
            if T == 0:
                tap("mixTa", mixT, [128, 8, TT])
        if stages <= 1:
            break
        bc4 = lambda col: col.un(2).bc([128, 4, 128])
        f2d = lambda tv: tv.re("p a t -> p (a t)")
        re4 = lambda ps_: ps_.re("p (a t) -> p a t", a=4)
        proj_group(12, 2, PBx)
        k.act(twal[0:64, :], PBx[0:64, 0, :], AF.Tanh)
        k.act(twal[64:128, :], PBx[64:128, 0, :], AF.Copy)
        k.act(sgl, PBx[:, 1, :], AF.Sigmoid)
        proj_group(0, 4, PBr)
        proj_group(4, 4, PBk)
        proj_group(8, 4, PBv)
        psw = nextbig()
        for p in range(4):
            k.mm(psw[:, p * 128:(p + 1) * 128], w2a2[0:64, p * 128:(p + 1) * 128], twal[0:64, :])
        psa_ = nextbig()
        for p in range(4):
            k.mm(psa_[:, p * 128:(p + 1) * 128], w2a2[64:128, p * 128:(p + 1) * 128], twal[64:128, :])
        k.tt("dve", sg, re4(psw), bc4(pw0), ALU.add)
        k.act(sg, sg, AF.Sigmoid)
        k.tt("dve", asig, re4(psa_), bc4(pa0), ALU.add)
        k.act(asig, asig, AF.Sigmoid)
        k.scan(f2d(cum), f2d(segm4), f2d(sg))
        k.act(Ex, cum, AF.Exp, scale=-math.exp(-0.5))
        k.tt("pool", AR[:, :, 0, 1, :], PBr, Ex, ALU.mult)
        k.copy("dve", PC[:, :, 0:1], Ex[:, :, 127:128])
        k.tt("dve", t1, cum, sg, ALU.subtract)
        k.tt("dve", kk, PBk, bc4(pkk), ALU.mult)
        k.act(t2, kk, AF.Square)
        pss = nextbig()
        for p in range(4):
            k.mm(pss[:, p * 128:(p + 1) * 128], blk2, t2[:, p, :])
        pe_fill(NFILL_A, pb[0])
        k.act(Ex, t1, AF.Exp, scale=-math.exp(-0.5))
        k.act(t2, re4(pss), AF.Sqrt)
        k.ts("dve", t2, t2, 1e-12, None, op0=ALU.max)
        k.recip(t2, t2)
        k.tt("dve", kk, kk, t2, ALU.mult)
        k.stt("dve", AR[:, :, 0, 0, :], kk, -1.0, Ex, op0=ALU.mult, op1=ALU.mult)
        k.stt("dve", t1, asig, -1.0, bc4(pka), op0=ALU.add, op1=ALU.mult)
        k.act(Ex, cum, AF.Exp, scale=math.exp(-0.5))
        k.ts("dve", t1, t1, 1.0, None, op0=ALU.add)
        k.tt("dve", kmod, PBk, t1, ALU.mult)
        k.tt("dve", t1, kk, asig, ALU.mult)
        k.tt("dve", BKT[:, :, 0, :], t1, Ex, ALU.mult)
        k.tt("pool", BKT[:, :, 1, :], kmod, Ex, ALU.mult)
        k.act(VT, PBv, AF.Copy)
        k.tt("pool", t2, PBr, kmod, ALU.mult)
        k.tt("pool", rkr, t2, bc4(prk), ALU.mult)
        if T == 0:
            tap("kk0", kk, [128, 4, 128])
            tap("kmod", kmod, [128, 4, 128])
            tap("asig", asig, [128, 4, 128])
            tap("rS", PBr, [128, 4, 128])
        if CUT == 1:
            return k, tapd
        for j in range(NCH):
            js = slice(j * 128, (j + 1) * 128)
            def tok_tr(src, dst):
                ptv_ = nextbig().bitcast(BF16).re("p (c t) -> p c t", c=8)
                for p_ in range(4):
                    k.tr(ptv_[:, p_, :], src[:, p_, js], ident)
                k.copy("dve", dst.re("p (c t) -> p c t", c=4), ptv_[:, 0:4, :])
            def pre_a():
                if T + 1 < ntiles and stages >= 3:
                    rms_A([xr[T + 1]], 128, [xsb[0]], rsN, junk_=junk2, ss_=ssN)

            def pre_b():
                flush_wout()
                if T + 1 < ntiles and stages >= 3:
                    rms_B(1, 128, G_MIX, hT, 0, [xsb[0]])
                    pre_rms[0] = True
            fill = {0: lambda: tok_tr(BKT[:, :, 0, :], Btok), 1: lambda: tok_tr(BKT[:, :, 1, :], Ktok),
                    2: lambda: tok_tr(VT, Vtok), 4: pre_a, 6: pre_b}
            if CUT == 2:
                return k, tapd
            for g in range(2):
                psa = [pb[0 + 3 * g], pb[1 + 3 * g]]
                hd = []
                for i in range(4):
                    h = 4 * g + i
                    hd.append((h // 2, slice((h % 2) * 64, (h % 2 + 1) * 64), i % 2, i // 2))
                for (p, hp, e, sl) in hd:
                    k.mm(psa[e][:, sl * 256:(sl + 1) * 256], BKT[hp, p, 0, js], AR[hp, p, j].re("p a t -> p (a t)"))
                for e in range(2):
                    pv = psa[e].re("p (i a t) -> p i a t", i=2, a=2)
                    k.tt("dve", MTP[g][:, e::2, 0, :], pv[:, :, 0, :], mask2[:, 0, :].un(1).bc([128, 2, 128]), ALU.mult)
                    k.tt("dve", LrbT[:, 4 * g + e:4 * g + 4:2, :], pv[:, :, 1, :], mask2[:, 1, :].un(1).bc([128, 2, 128]), ALU.mult)
                for (p, hp, e, sl) in hd:
                    k.mm(psa[e][:, sl * 256:(sl + 1) * 256], BKT[hp, p, 1, js], AR[hp, p, j].re("p a t -> p (a t)"))
                for e in range(2):
                    k.tt("dve", LT2[:, 4 * g + e:4 * g + 4:2, :], psa[e].re("p (i x) -> p i x", i=2), m2f.un(1).bc([128, 2, 256]), ALU.mult)
                for (p, hp, e, sl) in hd:
                    k.mm(psa[e][:, sl * 128:(sl + 1) * 128], AR[hp, p, j, 0, :], BKT[hp, p, 0, js])
                for e in range(2):
                    k.tt("dve", Mb[g][:, e::2, :], psa[e][:, 0:256].re("p (a t) -> p a t", a=2), maskN.un(1).bc([128, 2, 128]), ALU.mult)
                k.copy("dve", MTP[g][:, :, 1, :], ident.un(1).bc([128, 4, 128]))
            if CUT == 3:
                return k, tapd
            for g in range(2):
                k.mm(pb[2 + 3 * g], ident, I4.re("p a t -> p (a t)"), start=True, stop=True)
            for lev in range(7):
                for g in range(2):
                    psA, psB, psP = pb[0 + 3 * g], pb[1 + 3 * g], pb[2 + 3 * g]
                    mt = MTP[g]
                    if lev < 6:
                        for i in range(4):
                            k.mm(psA[:, i * 128:(i + 1) * 128], mt[:, i, 0, :], Mb[g][:, i, :])
                        for i in range(4):
                            k.mm(psB[:, i * 128:(i + 1) * 128], Mb[g][:, i, :], mt[:, i, 0, :])
                    for i in range(4):
                        k.mm(psP[:, i * 128:(i + 1) * 128], Mb[g][:, i, :], mt[:, i, 1, :], start=False, stop=True)
                    if lev < 6:
                        k.act(Mb[g], psA.re("p (a t) -> p a t", a=4), AF.Copy)
                        k.act(mt[:, :, 0, :], psB.re("p (a t) -> p a t", a=4), AF.Copy)
                    k.copy("dve", mt[:, :, 1, :], psP.re("p (a t) -> p a t", a=4))
                if lev in mixA_sched:
                    mixA_sched[lev]()
                if lev in fill:
                    fill[lev]()
            if CUT == 4:
                return k, tapd
            hbo = Hb[hb_i[0] % 2]
            hbn = Hb[(hb_i[0] + 1) % 2]
            hb_i[0] += 1
            psXe, psU, psH, psYe = [pb[0], pb[1]], pb[2], pb[3], [pb[4], pb[5]]
            for h in range(8):
                p, e = h // 2, h % 2
                hp = slice(e * 64, (e + 1) * 64)
                k.mm(psXe[e][:, p * 64:(p + 1) * 64], LT2[:, h, 0:128], Vtok[:, h * 64:(h + 1) * 64], start=True, stop=False)
                k.mm(psXe[e][:, p * 64:(p + 1) * 64], AR[hp, p, j, 0, :], hbo[hp, p, :], start=False, stop=True)
            for e in range(2):
                k.act(Xb[:, e::2, :], psXe[e][:, 0:256].re("p (q v) -> p q v", q=4), AF.Copy)
            for h in range(8):
                k.mm(psU[:, h * 64:(h + 1) * 64], MTP[h // 4][:, h % 4, 1, :], Xb[:, h, :])
            k.act(Ub, psU.re("p (h v) -> p h v", h=8), AF.Copy)
            for h in range(8):
                p, e = h // 2, h % 2
                hp = slice(e * 64, (e + 1) * 64)
                k.mm(psYe[e][:, p * 64:(p + 1) * 64], LrbT[:, h, :], Ub[:, h, :], start=True, stop=False)
                k.mm(psYe[e][:, p * 64:(p + 1) * 64], LT2[:, h, 128:256], Vtok[:, h * 64:(h + 1) * 64], start=False, stop=False)
                k.mm(psYe[e][:, p * 64:(p + 1) * 64], AR[hp, p, j, 1, :], hbo[hp, p, :], start=False, stop=True)
            for h in range(8):
                p = h // 2
                k.mm(psH[:, h * 64:(h + 1) * 64], Btok[:, p * 128:(p + 1) * 128], Ub[:, h, :], start=True, stop=False)
                k.mm(psH[:, h * 64:(h + 1) * 64], Ktok[:, p * 128:(p + 1) * 128], Vtok[:, h * 64:(h + 1) * 64], start=False, stop=True)
            psH4 = psH.re("p (q e v) -> p q e v", q=4, e=2)
            for e in range(2):
                hp = slice(e * 64, (e + 1) * 64)
                k.tt("dve", tmpH[hp], psH4[hp, :, e, :], Hs[hp], ALU.add)
                k.tt("dve", Hs[hp], tmpH[hp], PC[hp, :, j:j + 1].bc([64, 4, 64]), ALU.mult)
            k.copy("dve", hbn, Hs)
            if CUT == 5:
                return k, tapd
            for e in range(2):
                k.act(Ysb.re("p (q e v) -> p q e v", q=4, e=2)[:, :, e, :], psYe[e][:, 0:256].re("p (q v) -> p q v", q=4), AF.Copy)
            if T == 0 and j == 0:
                tap("Y", Ysb, [128, 512])
            ln_heads(128, Ysb, 8, 64, 64e-5, bct[:, 2, :], bct[:, 3, :], yn, sqs)
            psb = nextbig()
            for p in range(4):
                k.mm(psb[:, 2 * p:2 * p + 2], rkr[:, p, js], hsel)
            k.copy("dve", bon, psb[:, 0:8])
            k.tt("dve", sqs.re("p (h v) -> p h v", h=8), Vtok.re("p (h v) -> p h v", h=8), bon.un(2).bc([128, 8, 64]), ALU.mult)
            k.tt("dve", yn, yn, sqs, ALU.add)
            psg = nextbig()
            k.mm(psg, sgl[:, js], g2)
            pe_fill(NFILL_B, pb[0])
            k.tt("dve", ob, yn, psg, ALU.mult)
            ptv = nextbig().bitcast(BF16).re("p (c t) -> p c t", c=8)
            for p in range(4):
                k.tr(ptv[:, p, :], ob[:, p * 128:(p + 1) * 128], ident)
            k.copy("dve", mixT[:, 4:8, js], ptv[:, 0:4, :])
        if T == 0:
            tap("mixT", mixT, [128, 8, TT])
        if CUT == 6:
            return k, tapd
        pending_wout.append((T, mixT))
        if stages <= 2 or T == ntiles - 1:
            flush_wout()
        if T == 0:
            tap("x1", xr[0], [128, 1024])
        if T == 1:
            tap("x1b", xr[1], [128, 1024])
    if CUT == 7:
        return k, tapd
    if stages >= 2:
        k.dma("sp", O["shp_o"].re("(c p) -> p c", p=128), prevcol, noncontig=True)
        wko = A("wko", [64, 8, 64], F32)
        for e in range(2):
            hp = slice(e * 64, (e + 1) * 64)
            psT = pb[e].re("p (h k) -> p h k", h=8)
            for p in range(4):
                k.tr(psT[0:64, p, :], Hs[hp, p, :], identf[hp, hp])
            k.copy("dve", wko[:, e::2, :], psT[0:64, 0:4, :])
        k.dma("sp", O["wkvp_o"].re("h v k -> v h k"), wko)
    k.barrier()
    if stages >= 9:
        k.aoff = m_w1
        sample_S1()
        k.barrier()
    k.aoff = m0b
    if stages <= 3:
        return k, tapd

    wq = A("wq", [128, 8, 1024], BF16)
    wo = A("wo", [128, 8, 1024], BF16)
    if stages < 9:
        for c in range(8):
            k.dma("pool", wq[:, c, :], D["w_xq"].re("(c p) n -> c p n", p=128)[c])
        for c in range(8):
            k.dma("pool", wo[:, c, :], D["w_xo"].re("(c p) n -> c p n", p=128)[c])
    mkT = A("mkT", [128, 8, 256], BF16)
    mvb = A("mvb", [128, 2, 1024], BF16)
    m1b = k.aoff
    wk = A("wk", [128, 8, 1024], BF16)
    wv = A("wv", [128, 8, 1024], BF16)
    for c in range(8):
        k.dma("pool", wk[:, c, :], D["w_xk"].re("(c p) n -> c p n", p=128)[c])
    for c in range(8):
        k.dma("pool", wv[:, c, :], D["w_xv"].re("(c p) n -> c p n", p=128)[c])
    memx = [A("memx%d" % j, [128, 1024], F32) for j in range(2)]
    for j in range(2):
        k.dma("sp", memx[j], D["mem"].re("(j p) d -> j p d", p=128)[j])
    mnT = A("mnT", [128, 8, 256], BF16)
    rms_T(memx, 128, G_MEM, mnT, 0)
    mko = [A("mko%d" % j, [128, 1024], F32) for j in range(2)]
    mkb = A("mkb", [128, 1024], BF16)
    for which, (w, od) in enumerate(((wk, O["mk_o"]), (wv, O["mv_o"]))):
        for j in range(2):
            for hf in range(2):
                ps = nextbig()
                for c in range(8):
                    k.mm(ps, mnT[:, c, j * 128:(j + 1) * 128], w[:, c, hf * 512:(hf + 1) * 512], start=(c == 0), stop=(c == 7))
                k.act(mko[j][:, hf * 512:(hf + 1) * 512], ps, AF.Copy)
            k.dma("sp", od.re("(j p) d -> j p d", p=128)[j], mko[j])
            if which == 0:
                k.copy("dve", mkb, mko[j])
                ptv = nextbig().bitcast(BF16).re("p (c t) -> p c t", c=8)
                for c in range(8):
                    k.tr(ptv[:, c, :], mkb[:, c * 128:(c + 1) * 128], ident)
                k.copy("dve", mkT[:, :, j * 128:(j + 1) * 128], ptv)
            else:
                k.copy("dve", mvb[:, j, :], mko[j])
    k.barrier()
    k.aoff = m1b

    hT2s = [A("hT2_%d" % i, [128, 8, 512], BF16) for i in range(2)]
    qTs = [A("qT_%d" % i, [128, 8, 512], BF16) for i in range(2)]
    ef = [A("ef%d" % i, [128, 4, 256], BF16) for i in range(2)]
    Pn = [A("Pn%d" % i, [128, 4, 256], BF16) for i in range(2)]
    PT = A("PT", [128, 8, 512], BF16)
    oT = A("oT", [128, 8, 512], BF16)
    smx = [A("smx%d" % i, [128, 16], F32) for i in range(2)]

    def softmax_rows(pt, banks, nh_per_bank, M, e_out, p_out, sm):
        H = nh_per_bank * len(banks)
        mx, nmx, ssum, rinv = (sm[:pt, i * 4:i * 4 + H] for i in range(4))
        for b, psb_ in enumerate(banks):
            k.red("dve", mx[:, b * nh_per_bank:(b + 1) * nh_per_bank], psb_[:pt, 0:nh_per_bank * M].re("p (h m) -> p h m", h=nh_per_bank), op=ALU.max)
        k.ts("dve", nmx, mx, -1.0, None, op0=ALU.mult)
        k.memset("dve", ssum, 0.0)
        for h in range(H):
            b, hh = h // nh_per_bank, h % nh_per_bank
            k.act(e_out[:pt, h, :], banks[b][:pt, hh * M:(hh + 1) * M], AF.Exp, bias=nmx[:, h:h + 1], scale=1.0, accum=ssum[:, h:h + 1])
        k.recip(rinv, ssum)
        k.tt("dve", p_out[:pt], e_out[:pt], rinv.un(2).bc([pt, H, M]), ALU.mult)

    ngrp = 4 if (stages >= 5 and NT == 16) else 1
    ev = [0]
    nsub = 4 if NT >= 4 else 1
    NTOK = nsub * 128

    xb4 = [A("xb4_%d" % i, [128, 1024], BF16) for i in range(4)]
    rs4 = A("rs4", [128, 8], F32)

    def qprojA(G):
        rms_A([xr[4 * G + i] for i in range(nsub)], 128, xb4, rs4)

    def qproj(G):
        hT = hT2s[G % 2]
        qT = qTs[G % 2]
        rms_B(nsub, 128, G_X, hT, 0, xb4)
        for m in range(8):
            ps = pb[m % 4]
            for c in range(8):
                k.mm(ps[:, 0:NTOK], wq[:, c, m * 128:(m + 1) * 128], hT[:, c, 0:NTOK], start=(c == 0), stop=(c == 7))
            if m % 2 == 0:
                k.ts("dve", qT[:, m, 0:NTOK], ps[:, 0:NTOK], 0.0625, None, op0=ALU.mult)
            else:
                k.act(qT[:, m, 0:NTOK], ps[:, 0:NTOK], AF.Copy, scale=0.0625)

    oTs = [oT, A("oT_b", [128, 8, 512], BF16)]

    def scores(G, subs):
        qT = qTs[G % 2]
        for s_ in subs:
            pss = [pb[4 + 2 * (s_ % 2)], pb[5 + 2 * (s_ % 2)]]
            for h in range(4):
                for dc in range(2):
                    k.mm(pss[h // 2][:, (h % 2) * 256:(h % 2 + 1) * 256], qT[:, 2 * h + dc, s_ * 128:(s_ + 1) * 128], mkT[:, 2 * h + dc, :], start=(dc == 0), stop=(dc == 1))

    def smax(subs):
        for s_ in subs:
            pss = [pb[4 + 2 * (s_ % 2)], pb[5 + 2 * (s_ % 2)]]
            softmax_rows(128, pss, 2, 256, ef[s_ % 2], Pn[s_ % 2], smx[s_ % 2])

    def ptrans(subs):
        for s_ in subs:
            ptv = pb[s_ % 2].bitcast(BF16).re("p (c t) -> p c t", c=8)
            for h in range(4):
                for mt in range(2):
                    k.tr(ptv[:, h * 2 + mt, :], Pn[s_ % 2][:, h, mt * 128:(mt + 1) * 128], ident)
            k.copy("dve" if s_ % 2 == 0 else "act", PT[:, :, s_ * 128:(s_ + 1) * 128], ptv)

    def w_o(G):
        oTg = oTs[G % 2]
        for s_ in range(nsub):
            for hf in range(2):
                ps = pb[(2 * s_ + hf) % 4]
                for c in range(8):
                    k.mm(ps, oTg[:, c, s_ * 128:(s_ + 1) * 128], wo[:, c, hf * 512:(hf + 1) * 512], start=(c == 0), stop=(c == 7))
                xs_ = xr[4 * G + s_][:, hf * 512:(hf + 1) * 512]
                k.tt("dve", xs_, ps, xs_, ALU.add)

    qprojA(0)
    qproj(0)
    pend = None
    for G in range(ngrp):
        r1 = [s_ for s_ in (0, 1) if s_ < nsub]
        r2 = [s_ for s_ in (2, 3) if s_ < nsub]
        scores(G, r1)
        smax(r1)
        if pend is not None:
            w_o(pend)
        ptrans(r1)
        if G + 1 < ngrp:
            qprojA(G + 1)
        if r2:
            scores(G, r2)
            smax(r2)
        if G + 1 < ngrp:
            qproj(G + 1)
        if r2:
            ptrans(r2)
        oTg = oTs[G % 2]
        for c in range(8):
            h = c // 2
            ps = pb[c % 4]
            for mt in range(2):
                k.mm(ps[:, 0:NTOK], mvb[:, mt, c * 128:(c + 1) * 128], PT[:, h * 2 + mt, 0:NTOK], start=(mt == 0), stop=(mt == 1))
            if c % 2 == 0:
                k.act(oTg[:, c, 0:NTOK], ps[:, 0:NTOK], AF.Copy)
            else:
                k.copy("dve", oTg[:, c, 0:NTOK], ps[:, 0:NTOK])
        pend = G
    w_o(pend)
    tap("x2", xr[0], [128, 1024])
    k.barrier()
    if stages >= 9:
        k.aoff = m1b
        sample_S2()
        k.barrier()
    k.aoff = m0b
    if stages <= 5:
        return k, tapd

    NB = 8
    xnT = A("xnT", [128, 8, 2048], BF16)
    wu = [A("wu%d" % i, [128, 8, 512], BF16) for i in range(2)]
    wd = [A("wd%d" % i, [128, 4, 1024], BF16) for i in range(2)]
    hidT = [A("hidT%d" % i, [128, 4, 512], BF16) for i in range(2)]
    rl = [A("rl%d" % i, [128, 512], F32) for i in range(2)]

    hidT = [A("hidT%d" % i, [128, 4, 512], BF16) for i in range(2)]
    rl = [A("rl%d" % i, [128, 512], F32) for i in range(2)]

    def ffn_load(b):
        k.dma("pool", wu[b % 2], D["w_up"].re("(c p) n -> p c n", p=128)[:, :, b * 512:(b + 1) * 512])
        k.dma("pool", wd[b % 2], D["w_down"].re("(q p) n -> p q n", p=128)[:, b * 4:(b + 1) * 4, :])

    ffn_load(0)
    ffn_load(1)
    fgb = A("fgb", [128, 1024], F32)
    k.dma("sp", fgb, TV(D["final_g"].ap.partition_broadcast(128), None))
    yo = [A("yo%d" % i, [128, 1024], F32) for i in range(2)]
    ypv = O["y_p"].re("(j p) d -> j p d", p=128)

    def final_tile(T_):
        k.memset("pool", ss, 0.0)
        k.act(junk, xr[T_], AF.Square, accum=ss[:, 0:1])
        k.act(rs[:, 0:1], ss[:, 0:1], AF.Sqrt, bias=EPS, scale=1.0 / 1024)
        k.recip(rs[:, 0:1], rs[:, 0:1])
        k.stt("dve", yo[T_ % 2], xr[T_], rs[:, 0:1], fgb, op0=ALU.mult, op1=ALU.mult)
        k.dma("sp", ypv[T_], yo[T_ % 2])

    for G in range(NT // 4):
        rms_T([xr[4 * G + i] for i in range(4)], 128, G_FFN, xnT, G * 512)
    nblk = (NB if stages >= 7 else 1) if NT == 16 else 0
    sst = sample_ffn_prep() if stages >= 9 else None
    cu = [0]
    cd = [0]

    def ffn_up(b, tg):
        hid = hidT[(b * 4 + tg) % 2]
        for hc in range(4):
            ps = pb[cu[0] % 4]
            r_ = rl[cu[0] % 2]
            cu[0] += 1
            for c in range(8):
                k.mm(ps, wu[b % 2][:, c, hc * 128:(hc + 1) * 128], xnT[:, c, tg * 512:(tg + 1) * 512], start=(c == 0), stop=(c == 7))
            k.act(r_, ps, AF.Relu)
            k.act(hid[:, hc, :], r_, AF.Square)

    def ffn_down(b, tg):
        hid = hidT[(b * 4 + tg) % 2]
        for sub in range(4):
            T_ = tg * 4 + sub
            for hf in range(2):
                ps = pb[4 + cd[0] % 4]
                cd[0] += 1
                for hc in range(4):
                    k.mm(ps, hid[:, hc, sub * 128:(sub + 1) * 128], wd[b % 2][:, hc, hf * 512:(hf + 1) * 512], start=(hc == 0), stop=(hc == 3))
                xs_ = xr[T_][:, hf * 512:(hf + 1) * 512]
                k.tt("dve", xs_, ps, xs_, ALU.add)
            if b == nblk - 1 and nblk == NB:
                final_tile(T_)

    steps = [(b, tg) for b in range(nblk) for tg in range(4)]
    for i, (b, tg) in enumerate(steps):
        ffn_up(b, tg)
        if i > 0:
            ffn_down(*steps[i - 1])
            pb_, ptg = steps[i - 1]
            if ptg == 3:
                if sst is not None:
                    sample_ffn_block(sst, wu[pb_ % 2], wd[pb_ % 2])
                if pb_ + 2 < nblk:
                    ffn_load(pb_ + 2)
    if steps:
        ffn_down(*steps[-1])
        if sst is not None:
            sample_ffn_block(sst, wu[steps[-1][0] % 2], wd[steps[-1][0] % 2])
    tap("x3", xr[0], [128, 1024])

    if nblk != NB:
        for T in range(NT):
            final_tile(T)
    if stages >= 9:
        sample_final(fgb)
    k.barrier()
    k.aoff = m0b
    if stages <= 8:
        return k, tapd
    return k, tapd


def _in_maps(inputs, n=8):
    maps = []
    for c in range(n):
        m = {}
        m["xp"] = np.ascontiguousarray(inputs["x_prompt"][c])
        m["xs"] = np.ascontiguousarray(inputs["x_sample"][16 * c:16 * c + 16, 0])
        m["mem"] = np.ascontiguousarray(inputs["mem_prompt"][c])
        m["ck"] = np.ascontiguousarray(inputs["cache_mem_k"][0, 16 * c:16 * c + 16]).reshape(16, 256, 1024)
        m["cv"] = np.ascontiguousarray(inputs["cache_mem_v"][0, 16 * c:16 * c + 16]).reshape(16, 256, 1024)
        m["ssh"] = np.ascontiguousarray(inputs["state_shift"][0, 16 * c:16 * c + 16])
        m["swkv"] = np.ascontiguousarray(inputs["state_wkv"][0, 16 * c:16 * c + 16])
        for nme in W_NAMES:
            a = np.asarray(inputs[nme])
            if nme != "final_g":
                a = a[0]
            m[nme] = np.ascontiguousarray(a.reshape(W_SHAPES[nme]))
        maps.append(m)
    return maps


_NC_CACHE = {}


def kernel(**inputs):
    inputs = {k_: np.asarray(v_) for k_, v_ in inputs.items()}
    if "nc" not in _NC_CACHE:
        kb, _ = build()
        _NC_CACHE["nc"] = kb.finish()
    nc = _NC_CACHE["nc"]
    res = run_bass_kernel_spmd(nc, _in_maps(inputs), core_ids=list(range(8))).results
    cat = lambda n_: np.stack([np.asarray(r[n_]) for r in res])
    y_p = cat("y_p")
    y_s = cat("y_s").reshape(128, 1, 1024)
    mk = cat("mk_o").reshape(1, 8, 256, 4, 256)
    mv = cat("mv_o").reshape(1, 8, 256, 4, 256)
    shp = cat("shp_o").reshape(1, 8, 1792)
    wkvp = cat("wkvp_o").reshape(1, 8, 8, 64, 64)
    cvp = cat("cvp_o").reshape(1, 8, 128, 4, 128)
    shs = cat("shs_o").reshape(1, 128, 1792)
    wkvs = cat("wkvs_o").reshape(1, 128, 8, 64, 64)
    cvs = cat("cvs_o").reshape(1, 128, 1, 4, 128)
    return tuple(np.ascontiguousarray(a, dtype=np.float32) for a in (y_p, y_s, mk, mv, shp, wkvp, cvp, shs, wkvs, cvs))
```
